# Optimizing a Trainium2 kernel written in Bass

```python
import jax, jax.numpy as jnp
from jax import lax
import numpy as np

D_MODEL = 1024
BATCH = 8
SEQ = 4096
DEPTH = 1
DEC_BATCH = 128
DEC_SEQ = 4
PAST_LEN = 8192
PAGE_SIZE = 128

GROUPS = ((128, 1), (512, 4), (2048, 16))
N_GROUPS = len(GROUPS)
H_G = 8
HEAD_DIM = 64
ATT_W = H_G * HEAD_DIM
ROPE_THETA = 10000.0
BLOCK = 128
C_CONV = D_MODEL
CONV_WIDTH = 31
D_FF = 2816
N_ATT_COLS = N_GROUPS * 3 * ATT_W
N_IN = N_ATT_COLS + 2 * C_CONV + 2 * D_MODEL
RMS_EPS = 1e-6
LN_EPS = 1e-5
SCALE = HEAD_DIM ** -0.5

kernel_name = 'hybrid_dilated_attn_conformer_conv_macaron_step'


def rms_norm(x, g):
    xf = x.astype(jnp.float32)
    y = xf * lax.rsqrt(jnp.mean(xf * xf, axis=-1, keepdims=True) + RMS_EPS)
    return (y * g.astype(jnp.float32)).astype(x.dtype)


def layer_norm(x, g, b):
    xf = x.astype(jnp.float32)
    mu = jnp.mean(xf, axis=-1, keepdims=True)
    xc = xf - mu
    y = xc * lax.rsqrt(jnp.mean(xc * xc, axis=-1, keepdims=True) + LN_EPS)
    return (y * g.astype(jnp.float32) + b.astype(jnp.float32)).astype(x.dtype)


def half_ffn(x, norm_g, w_gate, w_up, w_down):
    h = rms_norm(x, norm_g)
    return x + 0.5 * ((jax.nn.silu(h @ w_gate) * (h @ w_up)) @ w_down)


def rope(x, positions):
    half = HEAD_DIM // 2
    inv_freq = ROPE_THETA ** (-jnp.arange(half, dtype=jnp.float32) * 2.0 / HEAD_DIM)
    ang = positions.astype(jnp.float32)[:, None] * inv_freq[None, :]
    cos = jnp.cos(ang)[None, :, None, :]
    sin = jnp.sin(ang)[None, :, None, :]
    xf = x.astype(jnp.float32)
    x1, x2 = xf[..., :half], xf[..., half:]
    return jnp.concatenate([x1 * cos - x2 * sin, x2 * cos + x1 * sin], axis=-1).astype(x.dtype)


def _to_class_blocks(t, dil, n_pad):
    b, s, h, e = t.shape
    n = s // dil
    t = t.reshape(b, n, dil, h, e).transpose(0, 2, 1, 3, 4)
    t = jnp.pad(t, ((0, 0), (0, 0), (0, n_pad - n), (0, 0), (0, 0)))
    return t.reshape(b, dil, n_pad // BLOCK, BLOCK, h, e)


def _with_prev_block(t):
    prev = jnp.pad(t[:, :, :-1], ((0, 0), (0, 0), (1, 0), (0, 0), (0, 0), (0, 0)))
    return jnp.concatenate([prev, t], axis=3)


def _from_class_blocks(t, s):
    b, dil = t.shape[:2]
    rest = t.shape[4:]
    t = t.reshape((b, dil, -1) + rest)[:, :, :s // dil]
    t = jnp.swapaxes(t, 1, 2)
    return t.reshape((b, s) + rest)


def dilated_attention_prompt(q, k, v, window, dil):
    s = q.shape[1]
    n = s // dil
    n_pad = -(-n // BLOCK) * BLOCK
    nb = n_pad // BLOCK
    span = window // dil
    qb = _to_class_blocks(q, dil, n_pad)
    kk = _with_prev_block(_to_class_blocks(k, dil, n_pad))
    vv = _with_prev_block(_to_class_blocks(v, dil, n_pad))
    sc = jnp.einsum('bdcqhe,bdckhe->bdchqk', qb, kk).astype(jnp.float32) * SCALE
    qi = jnp.arange(BLOCK)[:, None]
    kj = jnp.arange(2 * BLOCK)[None, :]
    dist = qi + BLOCK - kj
    key_sub = jnp.arange(nb)[:, None, None] * BLOCK - BLOCK + kj[None]
    valid = (dist >= 0) & (dist <= span) & (key_sub >= 0)
    sc = jnp.where(valid[None, None, :, None], sc, -jnp.inf)
    m = jnp.max(sc, axis=-1, keepdims=True)
    p = jnp.exp(sc - m)
    l = jnp.sum(p, axis=-1)
    o = jnp.einsum('bdchqk,bdckhe->bdcqhe', p, vv.astype(jnp.float32)) / jnp.swapaxes(l, 3, 4)[..., None]
    lse = jnp.swapaxes(m[..., 0] + jnp.log(l), 3, 4)
    return _from_class_blocks(o, s), _from_class_blocks(lse, s)


def dilated_attention_sample(q, k_all, v_all, window, dil):
    t = q.shape[1]
    wb = k_all.shape[1] - t
    dists = jnp.arange(window // dil + 1) * dil
    idx = wb + jnp.arange(t)[:, None] - dists[None, :]
    valid = idx >= 0
    idx = jnp.maximum(idx, 0)
    kg = k_all[:, idx]
    vg = v_all[:, idx]
    sc = jnp.einsum('bthe,btkhe->bthk', q, kg).astype(jnp.float32) * SCALE
    sc = jnp.where(valid[None, :, None, :], sc, -jnp.inf)
    m = jnp.max(sc, axis=-1, keepdims=True)
    p = jnp.exp(sc - m)
    l = jnp.sum(p, axis=-1)
    o = jnp.einsum('bthk,btkhe->bthe', p, vg.astype(jnp.float32)) / l[..., None]
    return o, m[..., 0] + jnp.log(l)


def depthwise_causal_conv(u_full, w, b):
    y = lax.conv_general_dilated(u_full, w[:, None, :], window_strides=(1,), padding='VALID',
                                 dimension_numbers=('NWC', 'WIO', 'NWC'),
                                 feature_group_count=u_full.shape[-1])
    return y + b


def token_mixer(xn, positions, kv_bufs, conv_buf, w_in, gate_bias, conv_w, conv_b, conv_ln_g, conv_ln_b,
                w_conv_out, w_att_out, w_o):
    b, t, _ = xn.shape
    proj = xn @ w_in
    qkv = proj[..., :N_ATT_COLS].reshape(b, t, N_GROUPS, 3, H_G, HEAD_DIM)
    glu_a, glu_b, gate_logits = jnp.split(proj[..., N_ATT_COLS:], [C_CONV, 2 * C_CONV], axis=-1)
    outs, lses, new_kv = [], [], []
    for g, (win, dil) in enumerate(GROUPS):
        q = rope(qkv[:, :, g, 0], positions)
        k = rope(qkv[:, :, g, 1], positions)
        v = qkv[:, :, g, 2]
        if kv_bufs is None:
            k_all, v_all = k, v
            o, lse = dilated_attention_prompt(q, k, v, win, dil)
        else:
            k_all = jnp.concatenate([kv_bufs[g][:, :, 0], k], axis=1)
            v_all = jnp.concatenate([kv_bufs[g][:, :, 1], v], axis=1)
            o, lse = dilated_attention_sample(q, k_all, v_all, win, dil)
        keep = min(win, k_all.shape[1])
        new_kv.append(jnp.stack([k_all[:, -keep:], v_all[:, -keep:]], axis=2))
        outs.append(o)
        lses.append(lse)
    w_grp = jax.nn.softmax(jnp.stack(lses, axis=0), axis=0)
    att = jnp.sum(w_grp[..., None] * jnp.stack(outs, axis=0), axis=0)
    a = att.astype(xn.dtype).reshape(b, t, ATT_W) @ w_att_out
    u = glu_a * jax.nn.sigmoid(glu_b)
    prev = jnp.zeros((b, CONV_WIDTH - 1, C_CONV), u.dtype) if conv_buf is None else conv_buf
    u_full = jnp.concatenate([prev, u], axis=1)
    c = depthwise_causal_conv(u_full, conv_w, conv_b)
    c = jax.nn.silu(layer_norm(c, conv_ln_g, conv_ln_b)) @ w_conv_out
    gates = jax.nn.sigmoid(gate_logits + gate_bias)
    g_a, g_c = gates[..., :D_MODEL], gates[..., D_MODEL:]
    y = (g_a * a + g_c * c) @ w_o
    return y, new_kv, u_full[:, -(CONV_WIDTH - 1):]


def setup_inputs(seed: int = 0) -> dict:
    key = jax.random.key(seed)
    ks = iter(jax.random.split(key, 32))

    def nrm(shape, scale):
        return jax.random.normal(next(ks), shape, jnp.float32) * scale

    inp = {}
    inp['x_prompt'] = nrm((BATCH, SEQ, D_MODEL), 1.0)
    inp['x_sample'] = nrm((DEC_BATCH, DEC_SEQ, D_MODEL), 1.0)
    for win, _ in GROUPS:
        inp['cache_kv_w%d' % win] = nrm((DEPTH, DEC_BATCH, min(win, PAST_LEN), 2, H_G, HEAD_DIM), 1.0)
    inp['state_conv'] = nrm((DEPTH, DEC_BATCH, CONV_WIDTH - 1, C_CONV), 0.5)
    inp['ffn1_norm'] = 1.0 + nrm((DEPTH, D_MODEL), 0.01)
    inp['ffn1_w_gate'] = nrm((DEPTH, D_MODEL, D_FF), D_MODEL ** -0.5)
    inp['ffn1_w_up'] = nrm((DEPTH, D_MODEL, D_FF), D_MODEL ** -0.5)
    inp['ffn1_w_down'] = nrm((DEPTH, D_FF, D_MODEL), D_FF ** -0.5)
    inp['mix_norm'] = 1.0 + nrm((DEPTH, D_MODEL), 0.01)
    inp['w_in'] = nrm((DEPTH, D_MODEL, N_IN), D_MODEL ** -0.5)
    inp['gate_bias'] = nrm((DEPTH, 2 * D_MODEL), 0.01)
    inp['conv_w'] = nrm((DEPTH, CONV_WIDTH, C_CONV), CONV_WIDTH ** -0.5)
    inp['conv_b'] = nrm((DEPTH, C_CONV), 0.01)
    inp['conv_ln_g'] = 1.0 + nrm((DEPTH, C_CONV), 0.01)
    inp['conv_ln_b'] = nrm((DEPTH, C_CONV), 0.01)
    inp['w_conv_out'] = nrm((DEPTH, C_CONV, D_MODEL), C_CONV ** -0.5)
    inp['w_att_out'] = nrm((DEPTH, ATT_W, D_MODEL), ATT_W ** -0.5)
    inp['w_o'] = nrm((DEPTH, D_MODEL, D_MODEL), D_MODEL ** -0.5)
    inp['ffn2_norm'] = 1.0 + nrm((DEPTH, D_MODEL), 0.01)
    inp['ffn2_w_gate'] = nrm((DEPTH, D_MODEL, D_FF), D_MODEL ** -0.5)
    inp['ffn2_w_up'] = nrm((DEPTH, D_MODEL, D_FF), D_MODEL ** -0.5)
    inp['ffn2_w_down'] = nrm((DEPTH, D_FF, D_MODEL), D_FF ** -0.5)
    inp['final_norm'] = 1.0 + nrm((D_MODEL,), 0.01)
    return inp


def reference(x_prompt, x_sample, cache_kv_w128, cache_kv_w512, cache_kv_w2048, state_conv,
              ffn1_norm, ffn1_w_gate, ffn1_w_up, ffn1_w_down,
              mix_norm, w_in, gate_bias, conv_w, conv_b, conv_ln_g, conv_ln_b, w_conv_out, w_att_out, w_o,
              ffn2_norm, ffn2_w_gate, ffn2_w_up, ffn2_w_down, final_norm):
    pos_p = jnp.arange(x_prompt.shape[1], dtype=jnp.int32)
    pos_s = PAST_LEN + jnp.arange(x_sample.shape[1], dtype=jnp.int32)
    caches = (cache_kv_w128, cache_kv_w512, cache_kv_w2048)
    xp, xs = x_prompt, x_sample
    kv_p = [[] for _ in GROUPS]
    kv_s = [[] for _ in GROUPS]
    conv_p, conv_s = [], []
    for l in range(DEPTH):
        mix_w = (w_in[l], gate_bias[l], conv_w[l], conv_b[l], conv_ln_g[l], conv_ln_b[l],
                 w_conv_out[l], w_att_out[l], w_o[l])
        ffn1 = (ffn1_norm[l], ffn1_w_gate[l], ffn1_w_up[l], ffn1_w_down[l])
        ffn2 = (ffn2_norm[l], ffn2_w_gate[l], ffn2_w_up[l], ffn2_w_down[l])
        xp = half_ffn(xp, *ffn1)
        xs = half_ffn(xs, *ffn1)
        hp, new_kv, new_conv = token_mixer(rms_norm(xp, mix_norm[l]), pos_p, None, None, *mix_w)
        xp = xp + hp
        for g in range(N_GROUPS):
            kv_p[g].append(new_kv[g])
        conv_p.append(new_conv)
        hs, new_kv, new_conv = token_mixer(rms_norm(xs, mix_norm[l]), pos_s, [c[l] for c in caches],
                                           state_conv[l], *mix_w)
        xs = xs + hs
        for g in range(N_GROUPS):
            kv_s[g].append(new_kv[g])
        conv_s.append(new_conv)
        xp = half_ffn(xp, *ffn2)
        xs = half_ffn(xs, *ffn2)
    y_prompt = rms_norm(xp, final_norm)
    y_sample = rms_norm(xs, final_norm)
    return (y_prompt, y_sample,
            jnp.stack(kv_p[0]), jnp.stack(kv_p[1]), jnp.stack(kv_p[2]), jnp.stack(conv_p),
            jnp.stack(kv_s[0]), jnp.stack(kv_s[1]), jnp.stack(kv_s[2]), jnp.stack(conv_s))
```

```python
import contextlib
import numpy as np
import ml_dtypes
import concourse.bass as bass
import concourse.mybir as mybir
from concourse.bass_utils import run_bass_kernel_spmd

F32 = mybir.dt.float32
BF16 = mybir.dt.bfloat16
ALU = mybir.AluOpType
AF = mybir.ActivationFunctionType
AX = mybir.AxisListType

D = 1024
KC = 8
DFF = 2816
FC = 22
T = 512
HP = 4
NG = 3
DILS = (1, 4, 16)
WINS = (128, 512, 2048)
CW = 31
PAST = 8192
NCORES = 8
WCH = 4096
NWBUF = 3
RMS_EPS = 1e-6
LN_EPS = 1e-5

P_FFN1, P_MIX, P_FFN2, P_FIN, P_CB, P_LNG, P_LNB, P_GB, P_CW = 0, 8, 16, 24, 32, 40, 48, 56, 72
NPAR = 72 + 8 * CW
NMASK = 1024
DBG = set()
CPYQ = "pool"


class _Stop(Exception):
    pass


def chk(name):
    if name in DBG:
        raise _Stop(name)


class Sched:
    def __init__(self, nc, stack):
        self.nc = nc
        self.stack = stack
        self.E = {"pe": nc.tensor, "act": nc.scalar, "dve": nc.vector, "pool": nc.gpsimd, "sp": nc.sync}
        self.sem = {e: stack.enter_context(nc.semaphore("s_" + e)) for e in self.E}
        self.cnt = {e: 0 for e in self.E}
        self.seen = {e: {} for e in self.E}
        self.lastw = {}
        self.reads = {}
        self.dsem = {}
        self.dcnt = {}
        self.dlast = {}
        self.nsem = 0

    def _wait(self, eng, tok):
        if tok is None:
            return
        name, val = tok
        if eng == "pe" and name == "pe":
            return
        if self.seen[eng].get(name, 0) >= val:
            return
        sem = self.sem[name] if name in self.sem else self.dsem[name]
        self.E[eng].wait_ge(sem, val)
        self.seen[eng][name] = val

    def _deps(self, eng, reads, writes):
        for k in reads:
            self._wait(eng, self.lastw.get(k))
        for k in writes:
            self._wait(eng, self.lastw.get(k))
            for tk in self.reads.get(k, ()):
                self._wait(eng, tk)

    def _commit(self, tok, reads, writes):
        for k in reads:
            self.reads.setdefault(k, []).append(tok)
        for k in writes:
            self.lastw[k] = tok
            self.reads[k] = []

    def op(self, eng, fn, reads=(), writes=()):
        reads = list(reads)
        writes = list(writes)
        self._deps(eng, reads, writes)
        ins = fn(self.E[eng])
        self.cnt[eng] += 1
        ins.then_inc(self.sem[eng], 1)
        self._commit((eng, self.cnt[eng]), reads, writes)

    def dma(self, q, key, out, in_, reads=(), writes=()):
        reads = list(reads)
        writes = list(writes)
        if key not in self.dsem:
            self.dsem[key] = self.stack.enter_context(self.nc.semaphore("d%d" % self.nsem))
            self.nsem += 1
            self.dcnt[key] = 0
        self._deps(q, reads, writes)
        self._wait(q, self.dlast.get(key))
        self.E[q].dma_start(out=out, in_=in_).then_inc(self.dsem[key], 16)
        self.dcnt[key] += 16
        tok = (key, self.dcnt[key])
        self.dlast[key] = tok
        self._commit(tok, reads, writes)
        return tok

    def dma_nosync(self, q, key, out, in_):
        if key not in self.dsem:
            self.dsem[key] = self.stack.enter_context(self.nc.semaphore("d%d" % self.nsem))
            self.nsem += 1
            self.dcnt[key] = 0
        self.E[q].dma_start(out=out, in_=in_).then_inc(self.dsem[key], 16)
        self.dcnt[key] += 16
        return (key, self.dcnt[key])

    def finish(self, eng="sp"):
        for key, c in self.dcnt.items():
            self._wait(eng, (key, c))
        for e in self.E:
            if e != eng and self.cnt[e] > 0:
                self._wait(eng, (e, self.cnt[e]))


class Arena:
    def __init__(self, base_ap, nbytes):
        self.base = base_ap
        self.nbytes = nbytes
        self.pos = 0

    def alloc(self, nelem, dtype):
        esz = 4 if dtype == F32 else 2
        nb = (nelem * esz + 63) // 64 * 64
        if nb >= 1024:
            self.pos = (self.pos + 1023) // 1024 * 1024
        off = self.pos
        self.pos += nb
        assert self.pos <= self.nbytes, ("arena overflow", self.pos, self.nbytes)
        v = self.base[:, off // 4:(off + nb) // 4]
        if dtype != F32:
            v = v.bitcast(dtype)
        return Buf(v[:, 0:nelem], off, esz)


class Buf:
    def __init__(self, ap, off, esz):
        self.ap = ap
        self.off = off
        self.esz = esz

    def pg(self, e0=0, n=None):
        if n is None:
            n = self.ap.shape[1] - e0
        b0 = self.off + e0 * self.esz
        b1 = self.off + (e0 + n) * self.esz
        return [("pg", p) for p in range(b0 // 1024, (b1 - 1) // 1024 + 1)]


def PS(b):
    return [("ps", b)]


def weight_stream():
    chunks = []

    def ffn(pref):
        c = []
        for i in range(11):
            blk = []
            for f in (2 * i, 2 * i + 1):
                blk.append(("n", pref + "_w_gate", KC, f * 128, ("g", f)))
                blk.append(("n", pref + "_w_up", KC, f * 128, ("u", f)))
            c.append(blk)
        for dc in range(8):
            c.append([("n", pref + "_w_down", FC, dc * 128, ("d", dc))])
        return c

    chunks += ffn("ffn1")
    n_ffn = len(chunks)
    natt = NG * 3 * 512
    for i in range(4):
        blk = []
        for c in (2 * i, 2 * i + 1):
            blk.append(("n", "w_in", KC, natt + c * 128, ("ga", c)))
            blk.append(("n", "w_in", KC, natt + D + c * 128, ("gb", c)))
        chunks.append(blk)
    for c in range(8):
        chunks.append([("c", "conv", CW, c, ("cv", c))])
    for g in range(NG):
        ocs = []
        for hp in range(HP):
            base = g * 1536
            ocs.append(("n", "w_in", KC, base + 512 + hp * 128, ("k", g, hp)))
            ocs.append(("s", "w_in", KC, base + 512 + hp * 128, ("ks", g, hp)))
            ocs.append(("n", "w_in", KC, base + 1024 + hp * 128, ("v", g, hp)))
            ocs.append(("n", "w_in", KC, base + hp * 128, ("q", g, hp)))
            ocs.append(("s", "w_in", KC, base + hp * 128, ("qs", g, hp)))
        for i in range(0, len(ocs), 4):
            chunks.append(ocs[i:i + 4])
    for dc in range(8):
        chunks.append([
            ("n", "w_in", KC, natt + 2 * D + dc * 128, ("gta", dc)),
            ("n", "w_in", KC, natt + 3 * D + dc * 128, ("gtc", dc)),
            ("n", "w_conv_out", KC, dc * 128, ("co", dc)),
            ("n", "w_att_out", 4, dc * 128, ("ao", dc)),
        ])
    for i in range(2):
        chunks.append([("n", "w_o", KC, dc * 128, ("wo", dc)) for dc in range(4 * i, 4 * i + 4)])
    n_mix = len(chunks) - n_ffn
    chunks += ffn("ffn2")
    return chunks, n_ffn, n_mix


WSHAPES = {
    "ffn1_w_gate": (D, DFF), "ffn1_w_up": (D, DFF), "ffn1_w_down": (DFF, D),
    "ffn2_w_gate": (D, DFF), "ffn2_w_up": (D, DFF), "ffn2_w_down": (DFF, D),
    "w_in": (D, 8704), "w_conv_out": (D, D), "w_att_out": (512, D), "w_o": (D, D),
}


def build(S, NB):
    NT = S // T
    NS = NB * 4
    nc = bass.Bass("TRN2", target_bir_lowering=False)
    stack = contextlib.ExitStack()

    def din(name, shape, dt=F32):
        return nc.dram_tensor(name, list(shape), dt, kind="ExternalInput").ap()

    def dout(name, shape):
        return nc.dram_tensor(name, list(shape), F32, kind="ExternalOutput").ap()

    x_d = din("x", (S, D))
    xs_d = din("xs", (NS, D))
    c_d = [din("c128", (NB, 128, 1024)), din("c512", (NB, 512, 1024)), din("c2048", (NB, 2048, 1024))]
    sconv_d = din("sconv", (NB, 30, D))
    W_d = {k: din(k, v) for k, v in WSHAPES.items()}
    par_d = din("params", (128, NPAR))
    ropeC_d = din("ropeC", (128, S + 64))
    ropeS_d = din("ropeS", (128, S + 64))
    masks_d = din("masks", (128, NMASK), BF16)
    ident_d = din("ident", (128, 128))
    zsel_d = din("zsel", (128, 127), BF16)
    ipad_d = din("ipad", (64, 67))
    vmask_d = din("vmask", (64, 4))
    g0mask_d = din("g0mask", (128, 32))

    KEEP = [min(w, S) for w in WINS]
    y_d = dout("y", (S, D))
    ys_d = dout("ys", (NS, D))
    kvp_d = [dout("kvp%d" % g, (KEEP[g], 1024)) for g in range(NG)]
    convp_d = dout("convp", (30, D))
    kvs_d = [dout("kvs%d" % g, (NB, WINS[g], 1024)) for g in range(NG)]
    convs_d = dout("convs", (NB, 30, D))

    chunks, n_ffn, n_mix = weight_stream()
    NCH = len(chunks)
    wscr = nc.dram_tensor("wscr", [NCH, 128, WCH], BF16).ap()
    qscr = nc.dram_tensor("qscr", [NS, NG * 512], BF16).ap()

    ARENA_KB = 207
    arena_t = stack.enter_context(nc.sbuf_tensor("arena", [128, ARENA_KB * 256], F32))
    AR = Arena(arena_t[:, :], ARENA_KB * 1024)
    psum = [stack.enter_context(nc.psum_tensor("ps%d" % b, [128, 512], F32)) for b in range(8)]
    sch = Sched(nc, stack)
    op = sch.op

    ident = AR.alloc(128, F32)
    identb = AR.alloc(128, BF16)
    onesb = AR.alloc(128, BF16)
    par = AR.alloc(NPAR, F32)
    masks = AR.alloc(NMASK, BF16)
    wbuf = [AR.alloc(WCH, BF16) for _ in range(NWBUF)]
    XT = AR.alloc(KC * T, F32)
    hT = AR.alloc(KC * T, BF16)
    rstd = AR.alloc(T, F32)
    uhist = AR.alloc(KC * 30, BF16)
    AR_ulast = AR.alloc(KC * 32, F32)
    XT3 = XT.ap.rearrange("p (k t) -> p k t", k=KC)
    hT3 = hT.ap.rearrange("p (k t) -> p k t", k=KC)
    uh3 = uhist.ap.rearrange("p (c t) -> p c t", c=KC)

    sch.dma("pool", "c0", ident.ap, ident_d[:, :], writes=ident.pg())
    sch.dma("pool", "c1", par.ap, par_d[:, :], writes=par.pg())
    sch.dma("pool", "c2", masks.ap, masks_d[:, :], writes=masks.pg())
    op("dve", lambda e: e.tensor_copy(out=identb.ap, in_=ident.ap), ident.pg(), identb.pg())
    op("dve", lambda e: e.memset(onesb.ap, 1.0), [], onesb.pg())
    op("dve", lambda e: e.memset(uhist.ap, 0.0), [], uhist.pg())

    def mask_view(g, first, m_blk):
        if g < 2:
            base = 256 if first else 0
            return masks.ap[:, base:base + 256].unsqueeze(1).broadcast_to([128, 2, 256])
        base = 512 + (4 * (1 if first else 0) + m_blk) * 64
        return masks.ap[:, base:base + 64].unsqueeze(1).broadcast_to([128, 8, 64])

    def phase_of(ci):
        return "preA" if ci < n_ffn else ("preB" if ci < n_ffn + n_mix else "preC")

    conv_chunk_ci = {}
    for ci, blk in enumerate(chunks):
        off = 0
        for (kind, src, kcn, col0, tag) in blk:
            if kind == "c":
                conv_chunk_ci[col0] = ci
                off += kcn * 128
                continue
            sv = W_d[src].rearrange("(kc p) o -> p kc o", p=128)
            dst = wscr[ci, :, off:off + kcn * 128].rearrange("p (kc o) -> p kc o", kc=kcn)
            if kind == "n":
                sch.dma_nosync("pool", phase_of(ci), dst, sv[:, :, col0:col0 + 128])
            else:
                for (d0, s0) in ((0, 32), (32, 0), (64, 96), (96, 64)):
                    sch.dma_nosync("pool", phase_of(ci), dst[:, :, d0:d0 + 32], sv[:, :, col0 + s0:col0 + s0 + 32])
            off += kcn * 128
    pre_tok = {k: (k, sch.dcnt[k]) for k in ("preA", "preB", "preC")}

    class WStream:
        def __init__(self):
            self.issued = 0
            self.total = None

        def ensure(self, upto):
            while self.issued <= upto and self.issued < self.total:
                li = self.issued
                ci = li % NCH
                slot = li % NWBUF
                n = sum(k[2] for k in chunks[ci]) * 128
                sch._wait("sp", pre_tok[phase_of(ci)])
                rd = [("dram", "cvw", chunks[ci][0][3])] if chunks[ci][0][0] == "c" else []
                sch.dma("sp", "w%d" % slot, wbuf[slot].ap[:, 0:n], wscr[ci, :, 0:n], reads=rd, writes=wbuf[slot].pg())
                self.issued += 1

        def get(self, li):
            self.ensure(li + NWBUF - 1)
            return wbuf[li % NWBUF]

    WS = WStream()
    WS.total = NCH * (NT + (1 if (NS > 0 and 'nosample' not in DBG) else 0))

    lin_rr = [0]
    LIN_N = [8]

    def lin_bank():
        b = lin_rr[0] % LIN_N[0]
        lin_rr[0] += 1
        return b

    local_mark = AR.pos
    sq = [AR.alloc(T, BF16), AR.alloc(T, BF16)]
    hid = AR.alloc(FC * T, BF16)
    sg = [AR.alloc(T, F32), AR.alloc(T, F32)]
    ffn_end = AR.pos
    xin_bufs = [AR.alloc(D, F32), AR.alloc(D, F32)]
    xin_end = AR.pos
    AR.pos = local_mark
    sq_m = [AR.alloc(T, BF16), AR.alloc(T, BF16)]
    assert sq_m[0].off == sq[0].off and sq_m[1].off == sq[1].off
    tmp = [AR.alloc(T, F32) for _ in range(4)]
    tmpA, tmpB, tmpC, tmpD = tmp
    rC = AR.alloc(T, F32)
    rS = AR.alloc(T, F32)
    QT = AR.alloc(NG * HP * T, BF16)
    QT4 = QT.ap.rearrange("p (g h t) -> p g h t", g=NG, h=HP)
    vTb = AR.alloc(T, BF16)
    kvst = AR.alloc(512, F32)
    uT = AR.alloc(KC * (30 + T), BF16)
    uT3 = uT.ap.rearrange("p (c t) -> p c t", c=KC)
    cpre = AR.alloc(KC * T, F32)
    cpre3 = cpre.ap.rearrange("p (c t) -> p c t", c=KC)
    mix_end = AR.pos
    attT = Buf(arena_t[:, rC.off // 4:(rC.off + 4096) // 4].bitcast(BF16), rC.off, 2)
    assert rS.off == rC.off + 2048
    attT3 = attT.ap.rearrange("p (h t) -> p h t", h=HP)
    cT = Buf(arena_t[:, uT.off // 4:(uT.off + 8192) // 4].bitcast(BF16), uT.off, 2)
    cT3 = cT.ap.rearrange("p (c t) -> p c t", c=KC)
    mT = Buf(arena_t[:, cpre.off // 4:(cpre.off + 8192) // 4].bitcast(BF16), cpre.off, 2)
    mT3 = mT.ap.rearrange("p (k t) -> p k t", k=KC)
    o2 = cpre.off + 8192
    PT = [Buf(arena_t[:, (o2 + i * 1024) // 4:(o2 + (i + 1) * 1024) // 4].bitcast(BF16), o2 + i * 1024, 2) for i in range(4)]
    numA = Buf(arena_t[:, (o2 + 4096) // 4:(o2 + 6144) // 4], o2 + 4096, 4)
    denA = Buf(arena_t[:, (o2 + 6144) // 4:(o2 + 8192) // 4], o2 + 6144, 4)
    AR.pos = max(ffn_end, xin_end, mix_end)
    kv_mark = AR.pos
    KT = [AR.alloc(HP * 5 * 128, BF16), AR.alloc(HP * 4 * 2 * 128, BF16), AR.alloc(HP * 16 * 2 * 128, BF16)]
    VV = [AR.alloc(5 * 512, BF16), AR.alloc(4 * 2 * 512, BF16), AR.alloc(16 * 2 * 512, BF16)]
    KT0v = KT[0].ap.rearrange("p (h s k) -> p h s k", h=HP, s=5)
    KT1v = KT[1].ap.rearrange("p (h r s k) -> p h r s k", h=HP, r=4, s=2)
    KT2v = KT[2].ap.rearrange("p (h r s k) -> p h r s k", h=HP, r=16, s=2)
    VV0v = VV[0].ap.rearrange("p (s c) -> p s c", s=5)
    VV1v = VV[1].ap.rearrange("p (r s c) -> p r s c", r=4, s=2)
    VV2v = VV[2].ap.rearrange("p (r s c) -> p r s c", r=16, s=2)
    print("arena: local %d..%d (ffn %d, xin %d, mix %d), prompt total %d of %d" %
          (local_mark, kv_mark, ffn_end, xin_end, mix_end, AR.pos, AR.nbytes))

    for g in range(NG):
        op("pool", lambda e, g=g: e.memset(KT[g].ap, 0.0), [], KT[g].pg())
        op("pool", lambda e, g=g: e.memset(VV[g].ap, 0.0), [], VV[g].pg())
    cstg = [Buf(arena_t[:, (cpre.off + i * 8192) // 4:(cpre.off + (i + 1) * 8192) // 4].bitcast(BF16), cpre.off + i * 8192, 2) for i in range(2)]
    for c in range(KC):
        stg = cstg[c % 2]
        for k in range(CW):
            op("pool", lambda e, k=k, c=c, stg=stg: e.tensor_scalar(
                out=stg.ap[:, k * 128:(k + 1) * 128], in0=ident.ap, scalar1=par.ap[:, P_CW + c * CW + k:P_CW + c * CW + k + 1],
                scalar2=None, op0=ALU.mult), ident.pg() + par.pg(), stg.pg(k * 128, 128))
        sch.dma("pool", "cvw%d" % (c % 2), wscr[conv_chunk_ci[c], :, 0:CW * 128], stg.ap[:, 0:CW * 128], reads=stg.pg(0, CW * 128),
                writes=[("dram", "cvw", c)])

    def load_x_tile(src_rows_ap, n_tok, stage=None, q="sp", key="xin"):
        nst = (n_tok + 127) // 128
        for st in range(nst):
            rows = min(128, n_tok - st * 128)
            xin = (stage or xin_bufs)[st % 2] if stage is None else stage
            sch.dma(q, "%s%d" % (key, st % 2), xin.ap[0:rows, :], src_rows_ap[st * 128:st * 128 + rows, :], writes=xin.pg())
            for h in range(2):
                b = lin_bank()
                for k4 in range(4):
                    kc = 4 * h + k4
                    op("pe", lambda e, kc=kc, k4=k4, b=b, rows=rows, xin=xin: e.transpose(
                        out=psum[b][:, k4 * 128:k4 * 128 + rows], in_=xin.ap[0:rows, kc * 128:(kc + 1) * 128],
                        identity=ident.ap[0:rows, 0:rows]), xin.pg() + ident.pg(), PS(b))
                src = psum[b][:, :].rearrange("p (k t) -> p k t", k=4)[:, :, 0:rows]
                dst = XT3[:, 4 * h:4 * h + 4, st * 128:st * 128 + rows]
                if h == 0:
                    op("act", lambda e, src=src, dst=dst: e.copy(out=dst, in_=src), PS(b), XT.pg())
                else:
                    op("dve", lambda e, src=src, dst=dst: e.tensor_copy(out=dst, in_=src), PS(b), XT.pg())

    def col_stats(src3, src_buf, n, func, scale, bias, dst):
        b = 4
        for kc in range(KC):
            s = sq[kc % 2]
            op("act", lambda e, kc=kc, s=s: e.activation(out=s.ap[:, 0:n], in_=src3[:, kc, 0:n], func=func),
               src_buf.pg(kc * T, n), s.pg())
            op("pe", lambda e, kc=kc, s=s: e.matmul(psum[b][:, 0:n], lhsT=onesb.ap, rhs=s.ap[:, 0:n], start=(kc == 0), stop=(kc == KC - 1)),
               onesb.pg() + s.pg(), PS(b))
        return b

    def rms_norm_to(dst3, dst_buf, gcol, n):
        b = col_stats(XT3, XT, n, AF.Square, None, None, None)
        op("act", lambda e: e.activation(out=rstd.ap[:, 0:n], in_=psum[b][:, 0:n], func=AF.Sqrt, bias=RMS_EPS, scale=1.0 / D), PS(b), rstd.pg())
        op("dve", lambda e: e.reciprocal(out=rstd.ap[:, 0:n], in_=rstd.ap[:, 0:n]), rstd.pg(), rstd.pg())
        for kc in range(KC):
            op("dve", lambda e, kc=kc: e.scalar_tensor_tensor(
                out=dst3[:, kc, 0:n], in0=XT3[:, kc, 0:n], scalar=par.ap[:, gcol + kc:gcol + kc + 1],
                in1=rstd.ap[:, 0:n], op0=ALU.mult, op1=ALU.mult),
               XT.pg(kc * T, n) + par.pg() + rstd.pg(), dst_buf.pg(kc * T, n))

    def mm_chunk(b, wb, woff, kcn, rhs3, rhs_buf, n):
        for kc in range(kcn):
            op("pe", lambda e, kc=kc: e.matmul(psum[b][:, 0:n], lhsT=wb.ap[:, woff + kc * 128:woff + (kc + 1) * 128],
                                              rhs=rhs3[:, kc, 0:n], start=(kc == 0), stop=(kc == kcn - 1)),
               wb.pg() + rhs_buf.pg(kc * T, n), PS(b))

    def ffn(li0, n):
        hid3 = hid.ap.rearrange("p (f t) -> p f t", f=FC)
        li = li0
        for i in range(11):
            wb = WS.get(li)
            li += 1
            for j, f in enumerate((2 * i, 2 * i + 1)):
                bg, bu = lin_bank(), lin_bank()
                mm_chunk(bg, wb, (2 * j) * 1024, KC, hT3, hT, n)
                mm_chunk(bu, wb, (2 * j + 1) * 1024, KC, hT3, hT, n)
                sgt = sg[f % 2]
                op("act", lambda e, bg=bg, sgt=sgt: e.activation(out=sgt.ap[:, 0:n], in_=psum[bg][:, 0:n], func=AF.Silu),
                   PS(bg), sgt.pg())
                op("dve", lambda e, bu=bu, sgt=sgt, f=f: e.tensor_tensor(out=hid3[:, f, 0:n], in0=psum[bu][:, 0:n],
                                                                       in1=sgt.ap[:, 0:n], op=ALU.mult),
                   PS(bu) + sgt.pg(), hid.pg(f * T, n))
        for dc in range(8):
            wb = WS.get(li)
            li += 1
            b = lin_bank()
            mm_chunk(b, wb, 0, FC, hid3, hid, n)
            op("dve", lambda e, b=b, dc=dc: e.scalar_tensor_tensor(out=XT3[:, dc, 0:n], in0=psum[b][:, 0:n], scalar=0.5,
                                                                 in1=XT3[:, dc, 0:n], op0=ALU.mult, op1=ALU.add),
               PS(b) + XT.pg(dc * T, n), XT.pg(dc * T, n))
        return li

    def store_y_tile(dst_rows_ap, n_tok, stage=None):
        rms_norm_to(XT3, XT, P_FIN, n_tok)
        nst = (n_tok + 127) // 128
        for st in range(nst):
            rows = min(128, n_tok - st * 128)
            yo = xin_bufs[st % 2] if stage is None else stage
            for h in range(2):
                b = lin_bank()
                for k4 in range(4):
                    kc = 4 * h + k4
                    op("pe", lambda e, kc=kc, k4=k4, b=b: e.transpose(
                        out=psum[b][0:rows, k4 * 128:(k4 + 1) * 128], in_=XT3[:, kc, st * 128:st * 128 + rows],
                        identity=ident.ap), XT.pg(kc * T + st * 128, rows) + ident.pg(), PS(b))
                if h == 0:
                    op("act", lambda e, b=b, yo=yo: e.copy(out=yo.ap[0:rows, 0:512], in_=psum[b][0:rows, :]), PS(b), yo.pg(0, 512))
                else:
                    op("dve", lambda e, b=b, yo=yo: e.tensor_copy(out=yo.ap[0:rows, 512:1024], in_=psum[b][0:rows, :]), PS(b), yo.pg(512, 512))
            sch.dma("pool", "yo%d" % (st % 2), dst_rows_ap[st * 128:st * 128 + rows, :], yo.ap[0:rows, :], reads=yo.pg())

    def glu_to_u(li, n, dst_fn):
        for i in range(4):
            wb = WS.get(li)
            li += 1
            for j, c in enumerate((2 * i, 2 * i + 1)):
                ba, bb = lin_bank(), lin_bank()
                mm_chunk(ba, wb, (2 * j) * 1024, KC, hT3, hT, n)
                mm_chunk(bb, wb, (2 * j + 1) * 1024, KC, hT3, hT, n)
                t = tmp[c % 2]
                dap, dkeys = dst_fn(c)
                op("act", lambda e, bb=bb, t=t: e.activation(out=t.ap[:, 0:n], in_=psum[bb][:, 0:n], func=AF.Sigmoid), PS(bb), t.pg())
                op("dve", lambda e, ba=ba, t=t, dap=dap: e.tensor_tensor(out=dap, in0=psum[ba][:, 0:n], in1=t.ap[:, 0:n], op=ALU.mult),
                   PS(ba) + t.pg(), dkeys)
        return li

    def conv_ops(n):
        for k in range(CW):
            for c in range(KC):
                wcol = par.ap[:, P_CW + c * CW + k:P_CW + c * CW + k + 1]
                rd = uT.pg(c * (30 + T) + k, n) + par.pg()
                wr = cpre.pg(c * T, n)
                if k == 0:
                    bcol = par.ap[:, P_CB + c:P_CB + c + 1]
                    yield ("dve", lambda e, c=c, wcol=wcol, bcol=bcol: e.tensor_scalar(
                        out=cpre3[:, c, 0:n], in0=uT3[:, c, 0:n], scalar1=wcol, scalar2=bcol, op0=ALU.mult, op1=ALU.add), rd, wr)
                else:
                    yield ("dve", lambda e, c=c, k=k, wcol=wcol: e.scalar_tensor_tensor(
                        out=cpre3[:, c, 0:n], in0=uT3[:, c, k:k + n], scalar=wcol, in1=cpre3[:, c, 0:n],
                        op0=ALU.mult, op1=ALU.add), rd + wr, wr)

    def conv_ln_silu(n):
        b = col_stats(cpre3, cpre, n, AF.Copy, None, None, None)
        op("dve", lambda e: e.tensor_scalar(out=tmpA.ap[:, 0:n], in0=psum[b][:, 0:n], scalar1=1.0 / D, scalar2=None, op0=ALU.mult),
           PS(b), tmpA.pg())
        for c in range(KC):
            op("dve", lambda e, c=c: e.tensor_tensor(out=cpre3[:, c, 0:n], in0=cpre3[:, c, 0:n], in1=tmpA.ap[:, 0:n], op=ALU.subtract),
               cpre.pg(c * T, n) + tmpA.pg(), cpre.pg(c * T, n))
        b = col_stats(cpre3, cpre, n, AF.Square, None, None, None)
        op("act", lambda e: e.activation(out=tmpB.ap[:, 0:n], in_=psum[b][:, 0:n], func=AF.Sqrt, bias=LN_EPS, scale=1.0 / D), PS(b), tmpB.pg())
        op("dve", lambda e: e.reciprocal(out=tmpB.ap[:, 0:n], in_=tmpB.ap[:, 0:n]), tmpB.pg(), tmpB.pg())
        for c in range(KC):
            op("dve", lambda e, c=c: e.tensor_tensor(out=cpre3[:, c, 0:n], in0=cpre3[:, c, 0:n], in1=tmpB.ap[:, 0:n], op=ALU.mult),
               cpre.pg(c * T, n) + tmpB.pg(), cpre.pg(c * T, n))
            op("act", lambda e, c=c: e.activation(out=cT3[:, c, 0:n], in_=cpre3[:, c, 0:n], func=AF.Silu,
                                                 bias=par.ap[:, P_LNB + c:P_LNB + c + 1], scale=par.ap[:, P_LNG + c:P_LNG + c + 1]),
               cpre.pg(c * T, n) + par.pg(), cT.pg(c * T, n))

    def conv_pe(li, n, rhs_fn=None):
        for c in range(KC):
            wb = WS.get(li)
            li += 1
            b = lin_bank()
            for k in range(CW):
                rhs = uT3[:, c, k:k + n] if rhs_fn is None else rhs_fn(c, k)
                rkeys = uT.pg(c * (30 + T) + k, n) if rhs_fn is None else conv_rhs_keys[0]
                outv = psum[b][:, 0:n] if rhs_fn is None else psum[b][:, 0:n].rearrange("p (b t) -> p b t", t=4)
                op("pe", lambda e, k=k, b=b, wb=wb, rhs=rhs, outv=outv: e.matmul(outv, lhsT=wb.ap[:, k * 128:(k + 1) * 128], rhs=rhs,
                                                                                 start=(k == 0), stop=(k == CW - 1)), wb.pg() + rkeys, PS(b))
            op("act", lambda e, c=c, b=b: e.activation(out=cpre3[:, c, 0:n], in_=psum[b][:, 0:n], func=AF.Identity,
                                                       bias=par.ap[:, P_CB + c:P_CB + c + 1]), PS(b) + par.pg(), cpre.pg(c * T, n))
        return li

    conv_rhs_keys = [None]

    def merge_and_out(li, n):
        for dc in range(8):
            wb = WS.get(li)
            li += 1
            b1, b2, b3, b4 = lin_bank(), lin_bank(), lin_bank(), lin_bank()
            mm_chunk(b1, wb, 0, KC, hT3, hT, n)
            mm_chunk(b2, wb, 1024, KC, hT3, hT, n)
            mm_chunk(b3, wb, 2048, KC, cT3, cT, n)
            mm_chunk(b4, wb, 3072, 4, attT3, attT, n)
            op("act", lambda e, dc=dc: e.activation(out=tmpA.ap[:, 0:n], in_=psum[b1][:, 0:n], func=AF.Sigmoid,
                                                   bias=par.ap[:, P_GB + dc:P_GB + dc + 1]), PS(b1) + par.pg(), tmpA.pg())
            op("act", lambda e, dc=dc: e.activation(out=tmpB.ap[:, 0:n], in_=psum[b2][:, 0:n], func=AF.Sigmoid,
                                                   bias=par.ap[:, P_GB + 8 + dc:P_GB + 8 + dc + 1]), PS(b2) + par.pg(), tmpB.pg())
            op("dve", lambda e: e.tensor_tensor(out=tmpA.ap[:, 0:n], in0=psum[b4][:, 0:n], in1=tmpA.ap[:, 0:n], op=ALU.mult),
               PS(b4) + tmpA.pg(), tmpA.pg())
            op("dve", lambda e: e.tensor_tensor(out=tmpB.ap[:, 0:n], in0=psum[b3][:, 0:n], in1=tmpB.ap[:, 0:n], op=ALU.mult),
               PS(b3) + tmpB.pg(), tmpB.pg())
            op("dve", lambda e, dc=dc: e.tensor_tensor(out=mT3[:, dc, 0:n], in0=tmpA.ap[:, 0:n], in1=tmpB.ap[:, 0:n], op=ALU.add),
               tmpA.pg() + tmpB.pg(), mT.pg(dc * T, n))
        for i in range(2):
            wb = WS.get(li)
            li += 1
            for j in range(4):
                dc = 4 * i + j
                b = lin_bank()
                mm_chunk(b, wb, j * 1024, KC, mT3, mT, n)
                op("dve", lambda e, b=b, dc=dc: e.tensor_tensor(out=XT3[:, dc, 0:n], in0=psum[b][:, 0:n], in1=XT3[:, dc, 0:n], op=ALU.add),
                   PS(b) + XT.pg(dc * T, n), XT.pg(dc * T, n))
        return li

    rope_rr = [0]

    def rope_chunk(bp, bs, n, dst_ap, dst_keys, f32_copy=None, runs=None):
        ta, tb = (tmpA, tmpB) if rope_rr[0] % 2 == 0 else (tmpC, tmpD)
        rope_rr[0] += 1
        op("dve", lambda e: e.tensor_tensor(out=ta.ap[:, 0:n], in0=psum[bp][:, 0:n], in1=rC.ap[:, 0:n], op=ALU.mult), PS(bp) + rC.pg(), ta.pg())
        op("dve", lambda e: e.tensor_tensor(out=tb.ap[:, 0:n], in0=psum[bs][:, 0:n], in1=rS.ap[:, 0:n], op=ALU.mult), PS(bs) + rS.pg(), tb.pg())
        if runs is not None:
            for (dap, c0, ncol) in runs:
                op("dve", lambda e, dap=dap, c0=c0, ncol=ncol: e.tensor_tensor(
                    out=dap, in0=ta.ap[:, c0:c0 + ncol].rearrange("p (u k) -> p u k", k=128),
                    in1=tb.ap[:, c0:c0 + ncol].rearrange("p (u k) -> p u k", k=128), op=ALU.add), ta.pg() + tb.pg(), dst_keys)
        elif f32_copy is None:
            op("dve", lambda e: e.tensor_tensor(out=dst_ap, in0=src_view(ta, n), in1=src_view(tb, n), op=ALU.add),
               ta.pg() + tb.pg(), dst_keys)
        else:
            op("dve", lambda e: e.tensor_tensor(out=ta.ap[:, 0:n], in0=ta.ap[:, 0:n], in1=tb.ap[:, 0:n], op=ALU.add), ta.pg() + tb.pg(), ta.pg())
            if dst_ap is not None:
                op("act", lambda e: e.copy(out=dst_ap, in_=src_view(ta, n)), ta.pg(), dst_keys)
        return ta

    view_d = [1]

    def src_view(buf, n):
        d = view_d[0]
        if d == 1:
            return buf.ap[:, 0:n]
        return buf.ap[:, 0:n].rearrange("p (i r) -> p i r", r=d)

    kv_deferred = []

    def emit_kv_out(g, j, hp, which, srcbuf):
        kv_deferred.append(lambda: emit_kv_out_now(g, j, hp, which, srcbuf))

    def flush_kv_out(keep=0):
        while len(kv_deferred) > keep:
            kv_deferred.pop(0)()

    def emit_kv_out_now(g, j, hp, which, srcbuf):
        pos0 = j * T
        first = S - KEEP[g]
        sts = [st for st in range(4) if pos0 + st * 128 >= first]
        if not sts:
            return
        st0 = sts[0]
        b = lin_bank()
        for st in sts:
            op("pe", lambda e, st=st, b=b: e.transpose(out=psum[b][:, st * 128:(st + 1) * 128], in_=srcbuf.ap[:, st * 128:(st + 1) * 128],
                                                        identity=ident.ap), srcbuf.pg() + ident.pg(), PS(b))
        op("act", lambda e, b=b: e.copy(out=kvst.ap[:, st0 * 128:512], in_=psum[b][:, st0 * 128:512]), PS(b), kvst.pg())
        r0 = pos0 + st0 * 128 - first
        c0 = which * 512 + hp * 128
        dst = kvp_d[g][r0:r0 + len(sts) * 128, c0:c0 + 128].rearrange("(s p) c -> p s c", p=128)
        sch.dma("pool", "kvst", dst, kvst.ap[:, st0 * 128:512].rearrange("p (s c) -> p s c", c=128), reads=kvst.pg())

    def emit_conv_out(ulast):
        ul3 = ulast.ap.rearrange("p (c t) -> p c t", c=KC)
        for h in range(2):
            b = lin_bank()
            stg = tmp[2 + h]
            for k4 in range(4):
                c = 4 * h + k4
                op("pe", lambda e, c=c, k4=k4, b=b: e.transpose(out=psum[b][0:30, k4 * 128:(k4 + 1) * 128], in_=ul3[:, c, 0:30],
                                                                  identity=ident.ap), ulast.pg() + ident.pg(), PS(b))
            op("act", lambda e, b=b, stg=stg: e.copy(out=stg.ap[0:30, :], in_=psum[b][0:30, :]), PS(b), stg.pg())
            sch.dma("pool", "convo%d" % h, convp_d[:, h * 512:(h + 1) * 512], stg.ap[0:30, :], reads=stg.pg())

    cpy_list = []
    for g in range(NG):
        for b in range(NB):
            cpy_list.append((kvs_d[g][b, 0:WINS[g] - 4, :], c_d[g][b, 4:WINS[g], :]))
    for b in range(NB):
        cpy_list.append((convs_d[b, 0:26, :], sconv_d[b, 4:30, :]))

    def emit_copies(k):
        if "nocopy" in DBG:
            return
        for _ in range(k):
            if cpy_list:
                o, i = cpy_list.pop(0)
                sch.dma_nosync(CPYQ, "cpy", o, i)

    try:
        chk('s_pre')
        li = 0
        per_tile_copies = (len(cpy_list) + max(NT, 1) - 1) // max(NT, 1)
        for j in range(NT):
            pos0 = j * T
            last_tile = (j == NT - 1)
            emit_copies(per_tile_copies)
            load_x_tile(x_d[pos0:pos0 + T, :], T)
            chk('s_load')
            rms_norm_to(hT3, hT, P_FFN1, T)
            chk('s_norm')
            li = ffn(li, T)
            chk('s_ffn1')
            rms_norm_to(hT3, hT, P_MIX, T)
            sch.dma("pool", "ropeC", rC.ap, ropeC_d[:, pos0:pos0 + T], writes=rC.pg())
            sch.dma("pool", "ropeS", rS.ap, ropeS_d[:, pos0:pos0 + T], writes=rS.pg())
            op("act", lambda e: e.copy(out=uT3[:, :, 0:30], in_=uh3), uhist.pg(), uT.pg())

            def u_dst(c):
                return uT3[:, c, 30:30 + T], uT.pg(c * (30 + T) + 30, T)

            if not last_tile:
                li = glu_to_u(li, T, u_dst)
            else:
                ul_keep = AR_ulast
                ul3k = ul_keep.ap.rearrange("p (c t) -> p c t", c=KC)
                for i in range(4):
                    wb = WS.get(li)
                    li += 1
                    for jj, c in enumerate((2 * i, 2 * i + 1)):
                        ba, bb = lin_bank(), lin_bank()
                        mm_chunk(ba, wb, (2 * jj) * 1024, KC, hT3, hT, T)
                        mm_chunk(bb, wb, (2 * jj + 1) * 1024, KC, hT3, hT, T)
                        t = tmp[c % 2]
                        op("act", lambda e, bb=bb, t=t: e.activation(out=t.ap, in_=psum[bb][:, :], func=AF.Sigmoid), PS(bb), t.pg())
                        op("dve", lambda e, ba=ba, t=t: e.tensor_tensor(out=t.ap, in0=psum[ba][:, :], in1=t.ap, op=ALU.mult), PS(ba) + t.pg(), t.pg())
                        op("act", lambda e, t=t, c=c: e.copy(out=uT3[:, c, 30:30 + T], in_=t.ap), t.pg(), uT.pg(c * (30 + T) + 30, T))
                        op("dve", lambda e, t=t, c=c: e.tensor_copy(out=ul3k[:, c, 0:30], in_=t.ap[:, T - 30:T]), t.pg(), ul_keep.pg())
            chk('s_glu')
            li = conv_pe(li, T)
            op("act", lambda e: e.copy(out=uh3, in_=uT3[:, :, T:T + 30]), uT.pg(), uhist.pg())
            cgen = iter(())
            conv_done = [False]

            def pump(k):
                for _ in range(k):
                    try:
                        eng, fn, rd, wr = next(cgen)
                    except StopIteration:
                        conv_done[0] = True
                        return
                    op(eng, fn, rd, wr)

            c_blk, m_blk = j // 4, j % 4
            for g in range(NG):
                d = DILS[g]
                view_d[0] = d
                kv_out = (pos0 + T) > (S - KEEP[g])
                pend = {}
                wb = None
                for oc in range(20):
                    if oc % 4 == 0:
                        wb = WS.get(li)
                        li += 1
                    hp, typ = oc // 5, oc % 5
                    b = lin_bank()
                    mm_chunk(b, wb, (oc % 4) * 1024, KC, hT3, hT, T)
                    pend[typ] = b
                    flush_kv_out(0)
                    if typ == 1:
                        if g == 0 and not kv_out:
                            runs = []
                            u = 0
                            while u < 4:
                                slot = (4 * j + u) % 5
                                ln = min(4 - u, 5 - slot)
                                runs.append((KT0v[:, hp, slot:slot + ln, :], u * 128, ln * 128))
                                u += ln
                            rope_chunk(pend[0], pend[1], T, None, KT[0].pg(), runs=runs)
                        elif g == 0:
                            ta = rope_chunk(pend[0], pend[1], T, None, None, f32_copy=True)
                            for u in range(4):
                                slot = (4 * j + u) % 5
                                op("act", lambda e, u=u, slot=slot, hp=hp, ta=ta: e.copy(out=KT0v[:, hp, slot, :], in_=ta.ap[:, u * 128:(u + 1) * 128]),
                                   ta.pg(), KT[0].pg())
                            emit_kv_out(g, j, hp, 0, ta)
                        else:
                            if g == 1:
                                dst = KT1v[:, hp, :, j % 2, :].rearrange("p r i -> p i r")
                            else:
                                dst = KT2v[:, hp, :, c_blk % 2, 32 * m_blk:32 * m_blk + 32].rearrange("p r i -> p i r")
                            if kv_out:
                                ta = rope_chunk(pend[0], pend[1], T, dst, KT[g].pg(), f32_copy=True)
                                emit_kv_out(g, j, hp, 0, ta)
                            else:
                                rope_chunk(pend[0], pend[1], T, dst, KT[g].pg())
                    elif typ == 4:
                        if g == 0:
                            dst = QT4[:, g, hp, :]
                        else:
                            dst = QT4[:, g, hp, :].rearrange("p (r i) -> p i r", r=d)
                        rope_chunk(pend[3], pend[4], T, dst, QT.pg((g * HP + hp) * T, T))
                    elif typ == 2:
                        op("act", lambda e, b=b: e.copy(out=vTb.ap, in_=psum[b][:, :]), PS(b), vTb.pg())
                        if kv_out:
                            tv = tmp[rope_rr[0] % 4]
                            op("act", lambda e, b=b, tv=tv: e.copy(out=tv.ap, in_=psum[b][:, :]), PS(b), tv.pg())
                            emit_kv_out(g, j, hp, 1, tv)
                        def vtrans(g=g, hp=hp, d=d):
                            if g < 2:
                                tb = lin_bank()
                                pb = psum[tb][:, :].bitcast(BF16)
                                for u in range(4):
                                    if g == 0:
                                        insl = vTb.ap[:, u * 128:(u + 1) * 128]
                                    else:
                                        insl = vTb.ap.rearrange("p (i r) -> p r i", r=d)[:, u, :]
                                    op("pe", lambda e, insl=insl, u=u, pb=pb: e.transpose(out=pb[:, u * 128:(u + 1) * 128], in_=insl, identity=identb.ap),
                                       vTb.pg() + identb.pg(), PS(tb))
                                if g == 0:
                                    for u in range(4):
                                        slot = (4 * j + u) % 5
                                        op("act", lambda e, u=u, slot=slot, hp=hp, pb=pb: e.copy(
                                            out=VV0v[:, slot, hp * 128:(hp + 1) * 128], in_=pb[:, u * 128:(u + 1) * 128]), PS(tb), VV[0].pg())
                                else:
                                    op("act", lambda e, hp=hp, pb=pb: e.copy(
                                        out=VV1v[:, :, j % 2, hp * 128:(hp + 1) * 128], in_=pb[:, 0:512].rearrange("p (r c) -> p r c", r=4)),
                                       PS(tb), VV[1].pg())
                            else:
                                vsrc = vTb.ap.rearrange("p (i r) -> p r i", r=d)
                                for half in range(2):
                                    tb = lin_bank()
                                    pb = psum[tb][:, :].bitcast(BF16)
                                    for r8 in range(8):
                                        r = half * 8 + r8
                                        op("pe", lambda e, r=r, r8=r8, pb=pb: e.transpose(out=pb[0:32, r8 * 128:(r8 + 1) * 128], in_=vsrc[:, r, :], identity=identb.ap),
                                           vTb.pg() + identb.pg(), PS(tb))
                                    op("act", lambda e, half=half, hp=hp, pb=pb: e.copy(
                                        out=VV2v[32 * m_blk:32 * m_blk + 32, half * 8:half * 8 + 8, c_blk % 2, hp * 128:(hp + 1) * 128],
                                        in_=pb[0:32, :].rearrange("p (r c) -> p r c", r=8)), PS(tb), VV[2].pg())

                        kv_deferred.append(vtrans)
            flush_kv_out(0)
            view_d[0] = 1
            chk('s_proj')
            while not conv_done[0]:
                pump(16)
            conv_ln_silu(T)
            chk('s_ln')
            if last_tile:
                emit_conv_out(AR_ulast)
            def units_of(g, sb):
                return [sb] if g < 2 else [4 * sb + k for k in range(4)]

            def kv_slices(hp, g, u, s, pc):
                if g == 0:
                    slot = (4 * j + u - 1 + pc) % 5
                    return (KT0v[s * 64:(s + 1) * 64, hp, slot, :], QT4[s * 64:(s + 1) * 64, g, hp, u * 128:(u + 1) * 128],
                            VV0v[:, slot, hp * 128 + s * 64:hp * 128 + (s + 1) * 64])
                if g == 1:
                    sl = (j + 1 + pc) % 2
                    return (KT1v[s * 64:(s + 1) * 64, hp, u, sl, :], QT4[s * 64:(s + 1) * 64, g, hp, u * 128:(u + 1) * 128],
                            VV1v[:, u, sl, hp * 128 + s * 64:hp * 128 + (s + 1) * 64])
                sl = (c_blk + 1 + pc) % 2
                return (KT2v[s * 64:(s + 1) * 64, hp, u, sl, :], QT4[s * 64:(s + 1) * 64, g, hp, u * 32:(u + 1) * 32],
                        VV2v[:, u, sl, hp * 128 + s * 64:hp * 128 + (s + 1) * 64])

            def colof(g, sbi, ui, pc):
                nun = 1 if g < 2 else 4
                nq = 128 if g < 2 else 32
                return ((sbi * nun + ui) * 2 + pc) * nq

            steps = [(hp, g, sp) for hp in range(HP) for g in range(NG) for sp in range(2)]

            def scores(k):
                hp, g, sp = steps[k]
                nq = 128 if g < 2 else 32
                for s in range(2):
                    bsx = 2 * (k % 2) + s
                    for sbi in range(2):
                        for ui, u in enumerate(units_of(g, 2 * sp + sbi)):
                            for pc in range(2):
                                kblk, qsl, _ = kv_slices(hp, g, u, s, pc)
                                col = colof(g, sbi, ui, pc)
                                op("pe", lambda e, kblk=kblk, qsl=qsl, col=col, bsx=bsx, nq=nq: e.matmul(
                                    psum[bsx][:, col:col + nq], lhsT=kblk, rhs=qsl, start=True, stop=True),
                                   KT[g].pg() + QT.pg((g * HP + hp) * T, T), PS(bsx))

            def softmax_part(k):
                hp, g, sp = steps[k]
                for s in range(2):
                    bsx = 2 * (k % 2) + s
                    pt = PT[2 * (k % 2) + s]
                    op("act", lambda e, bsx=bsx, pt=pt: e.activation(out=pt.ap, in_=psum[bsx][:, :], func=AF.Exp, scale=0.125), PS(bsx), pt.pg())
                    if g < 2:
                        for sbi in range(2):
                            sb = 2 * sp + sbi
                            first = (j == 0 and sb == 0) if g == 0 else (j == 0)
                            base = 256 if first else 0
                            pv2 = pt.ap[:, sbi * 256:(sbi + 1) * 256]
                            op("dve", lambda e, pv2=pv2, base=base: e.tensor_tensor(out=pv2, in0=pv2, in1=masks.ap[:, base:base + 256], op=ALU.mult),
                               pt.pg() + masks.pg(), pt.pg())
                    else:
                        mv = mask_view(g, c_blk == 0, m_blk)
                        pv3 = pt.ap.rearrange("p (a b) -> p a b", a=8)
                        op("dve", lambda e, pv3=pv3, mv=mv: e.tensor_tensor(out=pv3, in0=pv3, in1=mv, op=ALU.mult),
                           pt.pg() + masks.pg(), pt.pg())

            def nd_banks(hp, g):
                return (4, 5) if (hp * NG + g) % 2 == 0 else (6, 7)

            def pv(k):
                hp, g, sp = steps[k]
                nq = 128 if g < 2 else 32
                bn, bd = nd_banks(hp, g)
                for s in range(2):
                    pt = PT[2 * (k % 2) + s]
                    tp = None if s == 0 else (0, 64)
                    for sbi in range(2):
                        for ui, u in enumerate(units_of(g, 2 * sp + sbi)):
                            for pc in range(2):
                                _, _, vblk = kv_slices(hp, g, u, s, pc)
                                col = colof(g, sbi, ui, pc)
                                ocol = u * nq
                                op("pe", lambda e, vblk=vblk, pt=pt, col=col, ocol=ocol, s=s, pc=pc, tp=tp, bn=bn, nq=nq: e.matmul(
                                    psum[bn][s * 64:(s + 1) * 64, ocol:ocol + nq], lhsT=vblk, rhs=pt.ap[:, col:col + nq],
                                    start=(pc == 0), stop=(pc == 1), tile_position=tp),
                                   VV[g].pg() + pt.pg(), PS(bn))
                                op("pe", lambda e, pt=pt, col=col, ocol=ocol, s=s, pc=pc, tp=tp, bd=bd, nq=nq: e.matmul(
                                    psum[bd][s * 64:(s + 1) * 64, ocol:ocol + nq], lhsT=onesb.ap[:, 0:64], rhs=pt.ap[:, col:col + nq],
                                    start=(pc == 0), stop=(pc == 1), tile_position=tp),
                                   onesb.pg() + pt.pg(), PS(bd))

            def evac(hp, g):
                d = DILS[g]
                bn, bd = nd_banks(hp, g)
                for (bsrc, accb) in ((bn, numA), (bd, denA)):
                    if g == 0:
                        op("act", lambda e, bsrc=bsrc, accb=accb: e.copy(out=accb.ap, in_=psum[bsrc][:, :]), PS(bsrc), accb.pg())
                    else:
                        dstv = accb.ap.rearrange("p (i r) -> p r i", r=d)
                        srcv = psum[bsrc][:, :].rearrange("p (r i) -> p r i", r=d)
                        op("dve", lambda e, dstv=dstv, srcv=srcv: e.tensor_tensor(out=dstv, in0=srcv, in1=dstv, op=ALU.add),
                           PS(bsrc) + accb.pg(), accb.pg())

            def finalize(hp):
                op("dve", lambda e: e.reciprocal(out=denA.ap, in_=denA.ap), denA.pg(), denA.pg())
                op("dve", lambda e, hp=hp: e.tensor_tensor(out=attT3[:, hp, :], in0=numA.ap, in1=denA.ap, op=ALU.mult),
                   numA.pg() + denA.pg(), attT.pg(hp * T, T))

            pend_fin = []
            scores(0)
            for k, (hp, g, sp) in enumerate(steps):
                softmax_part(k)
                while pend_fin:
                    finalize(pend_fin.pop(0))
                if k + 1 < len(steps):
                    scores(k + 1)
                pv(k)
                if sp == 1:
                    evac(hp, g)
                    if g == NG - 1:
                        pend_fin.append(hp)
            while pend_fin:
                finalize(pend_fin.pop(0))
            chk('s_att')
            li = merge_and_out(li, T)
            chk('s_merge')
            rms_norm_to(hT3, hT, P_FFN2, T)
            li = ffn(li, T)
            store_y_tile(y_d[pos0:pos0 + T, :], T)
            chk('s_tile')

        emit_copies(len(cpy_list))
        if NS > 0 and 'nosample' not in DBG:
            n = NS
            AR.pos = kv_mark
            ucs = AR.alloc(KC * NB * 34, BF16)
            ucs4 = ucs.ap.rearrange("p (c b t) -> p c b t", c=KC, b=NB)
            uS = AR.alloc(KC * 64, F32)
            uS3 = uS.ap.rearrange("p (c t) -> p c t", c=KC)
            sstage = AR.alloc(D, F32)
            KVs = [AR.alloc(1024, F32) for _ in range(NG)]
            Qs = [AR.alloc(512, F32) for _ in range(NG)]
            Qb = AR.alloc(NG * 512, BF16)
            KVc = [AR.alloc(1024, F32), AR.alloc(4 * 1024, F32), AR.alloc(4 * 1024, F32)]
            Qbc = AR.alloc(4 * NG * 512, BF16)
            pvb = [AR.alloc(512, BF16), AR.alloc(512, BF16)]
            num_new = AR.alloc(512, F32)
            att_s = AR.alloc(512, F32)
            s_all = AR.alloc(96, F32)
            p_all = AR.alloc(96, F32)
            p_bf = AR.alloc(96, BF16)
            s_new = AR.alloc(48, F32)
            p_new = AR.alloc(48, F32)
            den_new = AR.alloc(8, F32)
            zsel = AR.alloc(127, BF16)
            ipad = AR.alloc(67, F32)
            vmask = AR.alloc(4, F32)
            g0mask = AR.alloc(32, F32)
            print("arena used (sample phase):", AR.pos, "of", AR.nbytes)
            sch.dma("pool", "c0", zsel.ap, zsel_d[:, :], writes=zsel.pg())
            sch.dma("pool", "c0", ipad.ap[0:64, :], ipad_d[:, :], writes=ipad.pg())
            sch.dma("pool", "c0", vmask.ap[0:64, :], vmask_d[:, :], writes=vmask.pg())
            sch.dma("pool", "c0", g0mask.ap, g0mask_d[:, :], writes=g0mask.pg())

            load_x_tile(xs_d, n)
            rms_norm_to(hT3, hT, P_FFN1, n)
            li = ffn(li, n)
            rms_norm_to(hT3, hT, P_MIX, n)
            sch.dma("pool", "ropeC", rC.ap[:, 0:64], ropeC_d[:, S:S + 64], writes=rC.pg())
            sch.dma("pool", "ropeS", rS.ap[:, 0:64], ropeS_d[:, S:S + 64], writes=rS.pg())
            li = glu_to_u(li, n, lambda c: (uS3[:, c, 0:n], uS.pg(c * 64, n)))
            for q0 in range(0, NB, 4):
                nb = min(4, NB - q0)
                rows = nb * 30
                sch.dma("sp", "sst", sstage.ap[0:rows, :], sconv_d[q0:q0 + nb, :, :].rearrange("b r c -> (b r) c"), writes=sstage.pg())
                for h in range(2):
                    b = lin_bank()
                    for k4 in range(4):
                        kc = 4 * h + k4
                        op("pe", lambda e, kc=kc, k4=k4, b=b, rows=rows: e.transpose(
                            out=psum[b][:, k4 * 128:k4 * 128 + rows], in_=sstage.ap[0:rows, kc * 128:(kc + 1) * 128],
                            identity=ident.ap[0:rows, 0:rows]), sstage.pg() + ident.pg(), PS(b))
                    for k4 in range(4):
                        kc = 4 * h + k4
                        op("dve", lambda e, kc=kc, k4=k4, b=b, nb=nb, q0=q0, rows=rows: e.tensor_copy(
                            out=ucs4[:, kc, q0:q0 + nb, 0:30], in_=psum[b][:, k4 * 128:k4 * 128 + rows].rearrange("p (b r) -> p b r", b=nb)),
                           PS(b), ucs.pg())
            op("dve", lambda e: e.tensor_copy(out=ucs4[:, :, :, 30:34], in_=uS3[:, :, 0:n].rearrange("p c (b t) -> p c b t", t=4)),
               uS.pg(), ucs.pg())
            for h in range(2):
                b = lin_bank()
                for k4 in range(4):
                    c = 4 * h + k4
                    op("pe", lambda e, c=c, k4=k4, b=b: e.transpose(out=psum[b][0:n, k4 * 128:(k4 + 1) * 128], in_=uS3[:, c, 0:n],
                                                                      identity=ident.ap), uS.pg() + ident.pg(), PS(b))
                op("act", lambda e, b=b, h=h: e.copy(out=sstage.ap[0:n, h * 512:(h + 1) * 512], in_=psum[b][0:n, :]), PS(b), sstage.pg())
            for bb in range(NB):
                sch.dma("pool", "convs", convs_d[bb, 26:30, :], sstage.ap[4 * bb:4 * bb + 4, :], reads=sstage.pg())
            conv_rhs_keys[0] = ucs.pg()
            li = conv_pe(li, n, rhs_fn=lambda c, k: ucs4[:, c, :, k:k + 4])
            view_d[0] = 1
            for g in range(NG):
                pend = {}
                wb = None
                for oc in range(20):
                    if oc % 4 == 0:
                        wb = WS.get(li)
                        li += 1
                    hp, typ = oc // 5, oc % 5
                    b = lin_bank()
                    mm_chunk(b, wb, (oc % 4) * 1024, KC, hT3, hT, n)
                    pend[typ] = b
                    if typ in (1, 4, 2):
                        if typ == 2:
                            ta = tmp[rope_rr[0] % 4]
                            op("act", lambda e, b=b, ta=ta: e.copy(out=ta.ap[:, 0:n], in_=psum[b][:, 0:n]), PS(b), ta.pg())
                        else:
                            ta = rope_chunk(pend[typ - 1], pend[typ], n, None, None, f32_copy=True)
                        tb = lin_bank()
                        op("pe", lambda e, tb=tb, ta=ta: e.transpose(out=psum[tb][0:n, 0:128], in_=ta.ap[:, 0:n], identity=ident.ap), ta.pg() + ident.pg(), PS(tb))
                        if typ == 1:
                            dstb, dsl = KVs[g], slice(hp * 128, (hp + 1) * 128)
                        elif typ == 2:
                            dstb, dsl = KVs[g], slice(512 + hp * 128, 512 + (hp + 1) * 128)
                        else:
                            dstb, dsl = Qs[g], slice(hp * 128, (hp + 1) * 128)
                        op("act", lambda e, tb=tb, dstb=dstb, dsl=dsl: e.copy(out=dstb.ap[0:n, dsl], in_=psum[tb][0:n, 0:128]), PS(tb), dstb.pg())
                for bb in range(NB):
                    sch.dma("pool", "kvsn", kvs_d[g][bb, WINS[g] - 4:WINS[g], :], KVs[g].ap[4 * bb:4 * bb + 4, :], reads=KVs[g].pg())
                op("dve", lambda e, g=g: e.tensor_copy(out=Qb.ap[0:n, g * 512:(g + 1) * 512], in_=Qs[g].ap[0:n, :]), Qs[g].pg(), Qb.pg())
            sch.dma("pool", "qscr", qscr[:, :], Qb.ap[0:n, :], reads=Qb.pg(), writes=[("dram", "qscr")])
            conv_ln_silu(n)

            LIN_N[0] = 4
            first_mm = [True]
            for bb in range(NB):
                sch.dma("sp", "kvc0", KVc[0].ap, c_d[0][bb, :, :], writes=KVc[0].pg())
                sch.dma("sp", "kvc1", KVc[1].ap.rearrange("p (t c) -> p t c", t=4), c_d[1][bb].rearrange("(i t) c -> i t c", t=4), writes=KVc[1].pg())
                sch.dma("sp", "kvc2", KVc[2].ap.rearrange("p (t c) -> p t c", t=4), c_d[2][bb].rearrange("(i r) c -> i r c", r=16)[:, 0:4, :], writes=KVc[2].pg())
                qsrc = qscr[4 * bb:4 * bb + 4, :].rearrange("t c -> (t c)").partition_broadcast(128)
                sch.dma("pool", "qbc", Qbc.ap, qsrc, reads=[("dram", "qscr")], writes=Qbc.pg())
                for g in range(NG):
                    for t in range(4):
                        pi = g * 4 + t
                        ksl = KVc[0].ap[:, 0:512] if g == 0 else KVc[g].ap[:, t * 1024:t * 1024 + 512]
                        qsl = Qbc.ap[:, (t * NG + g) * 512:(t * NG + g + 1) * 512]
                        tq = tmp[pi % 2]
                        op("dve", lambda e, ksl=ksl, qsl=qsl, tq=tq: e.tensor_tensor(out=tq.ap, in0=ksl, in1=qsl, op=ALU.mult),
                           KVc[g].pg() + Qbc.pg(), tq.pg())
                        op("dve", lambda e, pi=pi, tq=tq: e.tensor_reduce(out=s_all.ap[:, pi * 8:(pi + 1) * 8], in_=tq.ap.rearrange("p (h e) -> p h e", h=8),
                                                                          axis=AX.X, op=ALU.add), tq.pg(), s_all.pg())
                op("act", lambda e: e.activation(out=p_all.ap, in_=s_all.ap, func=AF.Exp, scale=0.125), s_all.pg(), p_all.pg())
                op("dve", lambda e: e.tensor_tensor(out=p_all.ap[:, 0:32], in0=p_all.ap[:, 0:32], in1=g0mask.ap, op=ALU.mult),
                   p_all.pg() + g0mask.pg(), p_all.pg())
                op("act", lambda e: e.copy(out=p_bf.ap, in_=p_all.ap), p_all.pg(), p_bf.pg())
                for g in range(NG):
                    for t in range(4):
                        pi = g * 4 + t
                        tok = 4 * bb + t
                        vsl = KVc[0].ap[:, 512:1024] if g == 0 else KVc[g].ap[:, t * 1024 + 512:(t + 1) * 1024]
                        pb_ = pvb[pi % 2]
                        pin = p_all.ap[:, pi * 8:(pi + 1) * 8].unsqueeze(2).broadcast_to([128, 8, 64])
                        op("pool", lambda e, vsl=vsl, pb_=pb_, pin=pin: e.tensor_tensor(out=pb_.ap.rearrange("p (h e) -> p h e", h=8),
                                                                                       in0=vsl.rearrange("p (h e) -> p h e", h=8), in1=pin, op=ALU.mult),
                           KVc[g].pg() + p_all.pg(), pb_.pg())
                        last = (bb == NB - 1 and pi == 11)
                        fm = first_mm[0]
                        first_mm[0] = False
                        lsel = zsel.ap[:, 63 - tok:63 - tok + n]
                        op("pe", lambda e, lsel=lsel, pb_=pb_, fm=fm, last=last: e.matmul(psum[6][0:n, :], lhsT=lsel, rhs=pb_.ap, start=fm, stop=last),
                           zsel.pg() + pb_.pg(), PS(6))
                        op("pe", lambda e, lsel=lsel, pi=pi, fm=fm, last=last: e.matmul(psum[7][0:n, 0:8], lhsT=lsel, rhs=p_bf.ap[:, pi * 8:(pi + 1) * 8],
                                                                                       start=fm, stop=last), zsel.pg() + p_bf.pg(), PS(7))
            vsh = [tmpA, tmpB, tmpD]
            for dl in range(1, 4):
                lsh = ipad.ap[0:n, 3 - dl:3 - dl + n]
                op("pe", lambda e, lsh=lsh: e.matmul(psum[0][0:n, :], lhsT=lsh, rhs=KVs[0].ap[0:n, 0:512], start=True, stop=True), ipad.pg() + KVs[0].pg(), PS(0))
                op("pe", lambda e, lsh=lsh: e.matmul(psum[1][0:n, :], lhsT=lsh, rhs=KVs[0].ap[0:n, 512:1024], start=True, stop=True), ipad.pg() + KVs[0].pg(), PS(1))
                col = 2 + dl
                op("dve", lambda e: e.tensor_tensor(out=tmpC.ap[0:n, :], in0=psum[0][0:n, :], in1=Qs[0].ap[0:n, :], op=ALU.mult), PS(0) + Qs[0].pg(), tmpC.pg())
                op("dve", lambda e, col=col: e.tensor_reduce(out=s_new.ap[0:n, col * 8:(col + 1) * 8], in_=tmpC.ap[0:n, :].rearrange("p (h e) -> p h e", h=8),
                                                             axis=AX.X, op=ALU.add), tmpC.pg(), s_new.pg())
                vst = vsh[dl - 1]
                op("act", lambda e, vst=vst: e.copy(out=vst.ap[0:n, :], in_=psum[1][0:n, :]), PS(1), vst.pg())
            for g in range(NG):
                op("dve", lambda e, g=g: e.tensor_tensor(out=tmpC.ap[0:n, :], in0=KVs[g].ap[0:n, 0:512], in1=Qs[g].ap[0:n, :], op=ALU.mult),
                   KVs[g].pg() + Qs[g].pg(), tmpC.pg())
                op("dve", lambda e, g=g: e.tensor_reduce(out=s_new.ap[0:n, g * 8:(g + 1) * 8], in_=tmpC.ap[0:n, :].rearrange("p (h e) -> p h e", h=8),
                                                         axis=AX.X, op=ALU.add), tmpC.pg(), s_new.pg())
            op("act", lambda e: e.activation(out=p_new.ap[0:n, :], in_=s_new.ap[0:n, :], func=AF.Exp, scale=0.125), s_new.pg(), p_new.pg())
            for dl in range(1, 4):
                col = 2 + dl
                op("dve", lambda e, col=col, dl=dl: e.tensor_scalar(out=p_new.ap[0:n, col * 8:(col + 1) * 8], in0=p_new.ap[0:n, col * 8:(col + 1) * 8],
                                                                   scalar1=vmask.ap[0:n, dl:dl + 1], scalar2=None, op0=ALU.mult),
                   p_new.pg() + vmask.pg(), p_new.pg())
            vsrcs = [(KVs[0].ap[0:n, 512:1024], KVs[0]), (KVs[1].ap[0:n, 512:1024], KVs[1]), (KVs[2].ap[0:n, 512:1024], KVs[2]),
                     (tmpA.ap[0:n, :], tmpA), (tmpB.ap[0:n, :], tmpB), (tmpD.ap[0:n, :], tmpD)]
            nn3 = num_new.ap[0:n, :].rearrange("p (h e) -> p h e", h=8)
            for ti, (vap, vbuf) in enumerate(vsrcs):
                pin = p_new.ap[0:n, ti * 8:(ti + 1) * 8].unsqueeze(2).broadcast_to([n, 8, 64])
                v3 = vap.rearrange("p (h e) -> p h e", h=8)
                if ti == 0:
                    op("dve", lambda e, v3=v3, pin=pin: e.tensor_tensor(out=nn3, in0=v3, in1=pin, op=ALU.mult), vbuf.pg() + p_new.pg(), num_new.pg())
                    op("dve", lambda e: e.tensor_copy(out=den_new.ap[0:n, :], in_=p_new.ap[0:n, 0:8]), p_new.pg(), den_new.pg())
                else:
                    t3 = tmpC.ap[0:n, :].rearrange("p (h e) -> p h e", h=8)
                    op("dve", lambda e, v3=v3, pin=pin, t3=t3: e.tensor_tensor(out=t3, in0=v3, in1=pin, op=ALU.mult), vbuf.pg() + p_new.pg(), tmpC.pg())
                    op("dve", lambda e: e.tensor_tensor(out=num_new.ap[0:n, :], in0=num_new.ap[0:n, :], in1=tmpC.ap[0:n, :], op=ALU.add),
                       num_new.pg() + tmpC.pg(), num_new.pg())
                    op("dve", lambda e, ti=ti: e.tensor_tensor(out=den_new.ap[0:n, :], in0=den_new.ap[0:n, :], in1=p_new.ap[0:n, ti * 8:(ti + 1) * 8], op=ALU.add),
                       den_new.pg() + p_new.pg(), den_new.pg())
            op("dve", lambda e: e.tensor_tensor(out=num_new.ap[0:n, :], in0=psum[6][0:n, :], in1=num_new.ap[0:n, :], op=ALU.add), PS(6) + num_new.pg(), num_new.pg())
            op("dve", lambda e: e.tensor_tensor(out=den_new.ap[0:n, :], in0=psum[7][0:n, 0:8], in1=den_new.ap[0:n, :], op=ALU.add), PS(7) + den_new.pg(), den_new.pg())
            op("dve", lambda e: e.reciprocal(out=den_new.ap[0:n, :], in_=den_new.ap[0:n, :]), den_new.pg(), den_new.pg())
            rin = den_new.ap[0:n, :].unsqueeze(2).broadcast_to([n, 8, 64])
            op("dve", lambda e: e.tensor_tensor(out=att_s.ap[0:n, :].rearrange("p (h e) -> p h e", h=8), in0=nn3, in1=rin, op=ALU.mult),
               num_new.pg() + den_new.pg(), att_s.pg())
            for hp in range(HP):
                tb = lin_bank()
                op("pe", lambda e, hp=hp, tb=tb: e.transpose(out=psum[tb][:, 0:n], in_=att_s.ap[0:n, hp * 128:(hp + 1) * 128], identity=ident.ap[0:n, 0:n]),
                   att_s.pg() + ident.pg(), PS(tb))
                op("act", lambda e, hp=hp, tb=tb: e.copy(out=attT3[:, hp, 0:n], in_=psum[tb][:, 0:n]), PS(tb), attT.pg(hp * T, n))
            LIN_N[0] = 8
            li = merge_and_out(li, n)
            rms_norm_to(hT3, hT, P_FFN2, n)
            li = ffn(li, n)
            store_y_tile(ys_d, n)

    except _Stop as ex:
        print('STOPPED AT', ex)
    sch.finish("sp")
    return nc, stack


def _consts(S):
    half = 32
    inv_freq = (np.float32(10000.0) ** (-(np.arange(half, dtype=np.float32) * np.float32(2.0) / np.float32(64)))).astype(np.float32)
    pos = np.concatenate([np.arange(S, dtype=np.float32), np.float32(PAST) + (np.arange(64) % 4).astype(np.float32)])
    ang = (pos[None, :] * inv_freq[:, None]).astype(np.float32)
    cos = np.cos(ang).astype(np.float32)
    sin = np.sin(ang).astype(np.float32)
    p = np.arange(128)
    f = p % 32
    sign = np.where((p % 64) < 32, -1.0, 1.0).astype(np.float32)
    ropeC = cos[f, :]
    ropeS = sin[f, :] * sign[:, None]
    kj = np.arange(128)[:, None]
    masks = np.zeros((128, NMASK), np.float32)
    qi = np.arange(128)[None, :]
    masks[:, 0:128] = (kj >= qi)
    masks[:, 128:256] = (kj <= qi)
    masks[:, 384:512] = (kj <= qi)
    qi32 = np.arange(32)[None, :]
    for var in range(2):
        for mb in range(4):
            base = 512 + (4 * var + mb) * 64
            if var == 0:
                masks[:, base:base + 32] = (kj >= 32 * mb + qi32)
            masks[:, base + 32:base + 64] = (kj <= 32 * mb + qi32)
    zsel = np.zeros((128, 127), np.float32)
    zsel[:, 63] = 1.0
    ipad = np.zeros((64, 67), np.float32)
    ipad[np.arange(64), np.arange(64) + 3] = 1.0
    tok = np.arange(64)
    vmask = ((tok % 4)[:, None] >= np.arange(4)[None, :]).astype(np.float32)
    g0mask = np.zeros((128, 4, 8), np.float32)
    for t in range(4):
        g0mask[:, t, :] = (np.arange(128) >= t)[:, None]
    return dict(ropeC=np.ascontiguousarray(ropeC), ropeS=np.ascontiguousarray(ropeS),
                masks=masks.astype(ml_dtypes.bfloat16), ident=np.eye(128, dtype=np.float32),
                zsel=zsel.astype(ml_dtypes.bfloat16), ipad=ipad, vmask=vmask, g0mask=g0mask.reshape(128, 32))


def _pack_params(inp):
    def col(v):
        return np.asarray(v, np.float32).reshape(8, 128).T
    par = np.zeros((128, NPAR), np.float32)
    par[:, P_FFN1:P_FFN1 + 8] = col(inp["ffn1_norm"][0])
    par[:, P_MIX:P_MIX + 8] = col(inp["mix_norm"][0])
    par[:, P_FFN2:P_FFN2 + 8] = col(inp["ffn2_norm"][0])
    par[:, P_FIN:P_FIN + 8] = col(inp["final_norm"])
    par[:, P_CB:P_CB + 8] = col(inp["conv_b"][0])
    par[:, P_LNG:P_LNG + 8] = col(inp["conv_ln_g"][0])
    par[:, P_LNB:P_LNB + 8] = col(inp["conv_ln_b"][0])
    gb = np.asarray(inp["gate_bias"][0], np.float32)
    par[:, P_GB:P_GB + 8] = col(gb[:D])
    par[:, P_GB + 8:P_GB + 16] = col(gb[D:])
    cw = np.asarray(inp["conv_w"][0], np.float32)
    par[:, P_CW:P_CW + 8 * CW] = cw.T.reshape(8, 128, CW).transpose(1, 0, 2).reshape(128, 8 * CW)
    return par


def run(inp, ncores=NCORES, trace=False):
    xp = np.asarray(inp["x_prompt"], np.float32)
    xsm = np.asarray(inp["x_sample"], np.float32)
    B, S, _ = xp.shape
    NBT = xsm.shape[0]
    assert B == ncores and NBT % ncores == 0
    NB = NBT // ncores
    nc, stack = build(S, NB)
    consts = _consts(S)
    par = _pack_params(inp)
    shared = {k: np.ascontiguousarray(np.asarray(inp[k], np.float32)[0]) for k in WSHAPES}
    shared["params"] = par
    shared.update(consts)
    caches = [np.asarray(inp["cache_kv_w128"], np.float32)[0], np.asarray(inp["cache_kv_w512"], np.float32)[0],
              np.asarray(inp["cache_kv_w2048"], np.float32)[0]]
    sconv = np.asarray(inp["state_conv"], np.float32)[0]
    in_maps = []
    for c in range(ncores):
        m = dict(shared)
        m["x"] = np.ascontiguousarray(xp[c])
        m["xs"] = np.ascontiguousarray(xsm[c * NB:(c + 1) * NB].reshape(NB * 4, D))
        for g, w in enumerate(WINS):
            m["c%d" % w] = np.ascontiguousarray(caches[g][c * NB:(c + 1) * NB].reshape(NB, w, 1024))
        m["sconv"] = np.ascontiguousarray(sconv[c * NB:(c + 1) * NB])
        in_maps.append(m)
    res = run_bass_kernel_spmd(nc, in_maps, core_ids=list(range(ncores)), trace=trace)
    stack.close()
    R = res.results
    KEEP = [min(w, S) for w in WINS]
    y = np.stack([R[c]["y"] for c in range(ncores)], 0)
    ys = np.concatenate([R[c]["ys"].reshape(NB, 4, D) for c in range(ncores)], 0)
    kvp = [np.stack([R[c]["kvp%d" % g].reshape(KEEP[g], 2, 8, 64) for c in range(ncores)], 0)[None] for g in range(NG)]
    convp = np.stack([R[c]["convp"] for c in range(ncores)], 0)[None]
    kvs = [np.concatenate([R[c]["kvs%d" % g].reshape(NB, WINS[g], 2, 8, 64) for c in range(ncores)], 0)[None] for g in range(NG)]
    convs = np.concatenate([R[c]["convs"] for c in range(ncores)], 0)[None]
    out = (y, ys, kvp[0], kvp[1], kvp[2], convp, kvs[0], kvs[1], kvs[2], convs)
    return out, res


def kernel(**inputs):
    out, _ = run(inputs)
    return tuple(np.ascontiguousarray(o.astype(np.float32, copy=False)) for o in out)
```

```python
import contextlib
import numpy as np
import ml_dtypes
import concourse.bass as bass
import concourse.mybir as mybir
from concourse.bass_utils import run_bass_kernel_spmd

F32 = mybir.dt.float32
BF16 = mybir.dt.bfloat16
ALU = mybir.AluOpType
AF = mybir.ActivationFunctionType
AX = mybir.AxisListType

D = 1024
KC = 8
DFF = 2816
FC = 22
T = 512
HP = 4
NG = 3
DILS = (1, 4, 16)
WINS = (128, 512, 2048)
CW = 31
PAST = 8192
NCORES = 8
WCH = 4096
NWBUF = 3
RMS_EPS = 1e-6
LN_EPS = 1e-5

P_FFN1, P_MIX, P_FFN2, P_FIN, P_CB, P_LNG, P_LNB, P_GB, P_CW = 0, 8, 16, 24, 32, 40, 48, 56, 72
NPAR = 72 + 8 * CW
NMASK = 1024
DBG = set()
CPYQ = "pool"


class _Stop(Exception):
    pass


def chk(name):
    if name in DBG:
        raise _Stop(name)


class Sched:
    def __init__(self, nc, stack):
        self.nc = nc
        self.stack = stack
        self.E = {"pe": nc.tensor, "act": nc.scalar, "dve": nc.vector, "pool": nc.gpsimd, "sp": nc.sync}
        self.sem = {e: stack.enter_context(nc.semaphore("s_" + e)) for e in self.E}
        self.cnt = {e: 0 for e in self.E}
        self.seen = {e: {} for e in self.E}
        self.lastw = {}
        self.reads = {}
        self.dsem = {}
        self.dcnt = {}
        self.dlast = {}
        self.nsem = 0

    def _wait(self, eng, tok):
        if tok is None:
            return
        name, val = tok
        if eng == "pe" and name == "pe":
            return
        if self.seen[eng].get(name, 0) >= val:
            return
        sem = self.sem[name] if name in self.sem else self.dsem[name]
        self.E[eng].wait_ge(sem, val)
        self.seen[eng][name] = val

    def _deps(self, eng, reads, writes):
        for k in reads:
            self._wait(eng, self.lastw.get(k))
        for k in writes:
            self._wait(eng, self.lastw.get(k))
            for tk in self.reads.get(k, ()):
                self._wait(eng, tk)

    def _commit(self, tok, reads, writes):
        for k in reads:
            self.reads.setdefault(k, []).append(tok)
        for k in writes:
            self.lastw[k] = tok
            self.reads[k] = []

    def op(self, eng, fn, reads=(), writes=()):
        reads = list(reads)
        writes = list(writes)
        self._deps(eng, reads, writes)
        ins = fn(self.E[eng])
        self.cnt[eng] += 1
        ins.then_inc(self.sem[eng], 1)
        self._commit((eng, self.cnt[eng]), reads, writes)

    def dma(self, q, key, out, in_, reads=(), writes=()):
        reads = list(reads)
        writes = list(writes)
        if key not in self.dsem:
            self.dsem[key] = self.stack.enter_context(self.nc.semaphore("d%d" % self.nsem))
            self.nsem += 1
            self.dcnt[key] = 0
        self._deps(q, reads, writes)
        self._wait(q, self.dlast.get(key))
        self.E[q].dma_start(out=out, in_=in_).then_inc(self.dsem[key], 16)
        self.dcnt[key] += 16
        tok = (key, self.dcnt[key])
        self.dlast[key] = tok
        self._commit(tok, reads, writes)
        return tok

    def dma_nosync(self, q, key, out, in_):
        if key not in self.dsem:
            self.dsem[key] = self.stack.enter_context(self.nc.semaphore("d%d" % self.nsem))
            self.nsem += 1
            self.dcnt[key] = 0
        self.E[q].dma_start(out=out, in_=in_).then_inc(self.dsem[key], 16)
        self.dcnt[key] += 16
        return (key, self.dcnt[key])

    def finish(self, eng="sp"):
        for key, c in self.dcnt.items():
            self._wait(eng, (key, c))
        for e in self.E:
            if e != eng and self.cnt[e] > 0:
                self._wait(eng, (e, self.cnt[e]))


class Arena:
    def __init__(self, base_ap, nbytes):
        self.base = base_ap
        self.nbytes = nbytes
        self.pos = 0

    def alloc(self, nelem, dtype):
        esz = 4 if dtype == F32 else 2
        nb = (nelem * esz + 63) // 64 * 64
        if nb >= 1024:
            self.pos = (self.pos + 1023) // 1024 * 1024
        off = self.pos
        self.pos += nb
        assert self.pos <= self.nbytes, ("arena overflow", self.pos, self.nbytes)
        v = self.base[:, off // 4:(off + nb) // 4]
        if dtype != F32:
            v = v.bitcast(dtype)
        return Buf(v[:, 0:nelem], off, esz)


class Buf:
    def __init__(self, ap, off, esz):
        self.ap = ap
        self.off = off
        self.esz = esz

    def pg(self, e0=0, n=None):
        if n is None:
            n = self.ap.shape[1] - e0
        b0 = self.off + e0 * self.esz
        b1 = self.off + (e0 + n) * self.esz
        return [("pg", p) for p in range(b0 // 1024, (b1 - 1) // 1024 + 1)]


def PS(b):
    return [("ps", b)]


def weight_stream():
    chunks = []

    def ffn(pref):
        c = []
        for i in range(11):
            blk = []
            for f in (2 * i, 2 * i + 1):
                blk.append(("n", pref + "_w_gate", KC, f * 128, ("g", f)))
                blk.append(("n", pref + "_w_up", KC, f * 128, ("u", f)))
            c.append(blk)
        for dc in range(8):
            c.append([("n", pref + "_w_down", FC, dc * 128, ("d", dc))])
        return c

    chunks += ffn("ffn1")
    n_ffn = len(chunks)
    natt = NG * 3 * 512
    for i in range(4):
        blk = []
        for c in (2 * i, 2 * i + 1):
            blk.append(("n", "w_in", KC, natt + c * 128, ("ga", c)))
            blk.append(("n", "w_in", KC, natt + D + c * 128, ("gb", c)))
        chunks.append(blk)
    for c in range(8):
        chunks.append([("c", "conv", CW, c, ("cv", c))])
    for g in range(NG):
        ocs = []
        for hp in range(HP):
            base = g * 1536
            ocs.append(("n", "w_in", KC, base + 512 + hp * 128, ("k", g, hp)))
            ocs.append(("s", "w_in", KC, base + 512 + hp * 128, ("ks", g, hp)))
            ocs.append(("n", "w_in", KC, base + 1024 + hp * 128, ("v", g, hp)))
            ocs.append(("n", "w_in", KC, base + hp * 128, ("q", g, hp)))
            ocs.append(("s", "w_in", KC, base + hp * 128, ("qs", g, hp)))
        for i in range(0, len(ocs), 4):
            chunks.append(ocs[i:i + 4])
    for dc in range(8):
        chunks.append([
            ("n", "w_in", KC, natt + 2 * D + dc * 128, ("gta", dc)),
            ("n", "w_in", KC, natt + 3 * D + dc * 128, ("gtc", dc)),
            ("n", "w_conv_out", KC, dc * 128, ("co", dc)),
            ("n", "w_att_out", 4, dc * 128, ("ao", dc)),
        ])
    for i in range(2):
        chunks.append([("n", "w_o", KC, dc * 128, ("wo", dc)) for dc in range(4 * i, 4 * i + 4)])
    n_mix = len(chunks) - n_ffn
    chunks += ffn("ffn2")
    return chunks, n_ffn, n_mix


WSHAPES = {
    "ffn1_w_gate": (D, DFF), "ffn1_w_up": (D, DFF), "ffn1_w_down": (DFF, D),
    "ffn2_w_gate": (D, DFF), "ffn2_w_up": (D, DFF), "ffn2_w_down": (DFF, D),
    "w_in": (D, 8704), "w_conv_out": (D, D), "w_att_out": (512, D), "w_o": (D, D),
}


def build(S, NB):
    NT = S // T
    NS = NB * 4
    nc = bass.Bass("TRN2", target_bir_lowering=False)
    stack = contextlib.ExitStack()

    def din(name, shape, dt=F32):
        return nc.dram_tensor(name, list(shape), dt, kind="ExternalInput").ap()

    def dout(name, shape):
        return nc.dram_tensor(name, list(shape), F32, kind="ExternalOutput").ap()

    x_d = din("x", (S, D))
    xs_d = din("xs", (NS, D))
    c_d = [din("c128", (NB, 128, 1024)), din("c512", (NB, 512, 1024)), din("c2048", (NB, 2048, 1024))]
    sconv_d = din("sconv", (NB, 30, D))
    W_d = {k: din(k, v) for k, v in WSHAPES.items()}
    par_d = din("params", (128, NPAR))
    ropeC_d = din("ropeC", (128, S + 64))
    ropeS_d = din("ropeS", (128, S + 64))
    masks_d = din("masks", (128, NMASK), BF16)
    ident_d = din("ident", (128, 128))
    zsel_d = din("zsel", (128, 127), BF16)
    ipad_d = din("ipad", (64, 67))
    vmask_d = din("vmask", (64, 4))
    g0mask_d = din("g0mask", (128, 32))

    KEEP = [min(w, S) for w in WINS]
    y_d = dout("y", (S, D))
    ys_d = dout("ys", (NS, D))
    kvp_d = [dout("kvp%d" % g, (KEEP[g], 1024)) for g in range(NG)]
    convp_d = dout("convp", (30, D))
    kvs_d = [dout("kvs%d" % g, (NB, WINS[g], 1024)) for g in range(NG)]
    convs_d = dout("convs", (NB, 30, D))

    chunks, n_ffn, n_mix = weight_stream()
    NCH = len(chunks)
    wscr = nc.dram_tensor("wscr", [NCH, 128, WCH], BF16).ap()
    qscr = nc.dram_tensor("qscr", [NS, NG * 512], BF16).ap()

    ARENA_KB = 207
    arena_t = stack.enter_context(nc.sbuf_tensor("arena", [128, ARENA_KB * 256], F32))
    AR = Arena(arena_t[:, :], ARENA_KB * 1024)
    psum = [stack.enter_context(nc.psum_tensor("ps%d" % b, [128, 512], F32)) for b in range(8)]
    sch = Sched(nc, stack)
    op = sch.op

    ident = AR.alloc(128, F32)
    identb = AR.alloc(128, BF16)
    onesb = AR.alloc(128, BF16)
    par = AR.alloc(NPAR, F32)
    masks = AR.alloc(NMASK, BF16)
    wbuf = [AR.alloc(WCH, BF16) for _ in range(NWBUF)]
    XT = AR.alloc(KC * T, F32)
    hT = AR.alloc(KC * T, BF16)
    rstd = AR.alloc(T, F32)
    uhist = AR.alloc(KC * 30, BF16)
    AR_ulast = AR.alloc(KC * 32, F32)
    XT3 = XT.ap.rearrange("p (k t) -> p k t", k=KC)
    hT3 = hT.ap.rearrange("p (k t) -> p k t", k=KC)
    uh3 = uhist.ap.rearrange("p (c t) -> p c t", c=KC)

    sch.dma("pool", "c0", ident.ap, ident_d[:, :], writes=ident.pg())
    sch.dma("pool", "c1", par.ap, par_d[:, :], writes=par.pg())
    sch.dma("pool", "c2", masks.ap, masks_d[:, :], writes=masks.pg())
    op("dve", lambda e: e.tensor_copy(out=identb.ap, in_=ident.ap), ident.pg(), identb.pg())
    op("dve", lambda e: e.memset(onesb.ap, 1.0), [], onesb.pg())
    op("dve", lambda e: e.memset(uhist.ap, 0.0), [], uhist.pg())

    def mask_view(g, first, m_blk):
        if g < 2:
            base = 256 if first else 0
            return masks.ap[:, base:base + 256].unsqueeze(1).broadcast_to([128, 2, 256])
        base = 512 + (4 * (1 if first else 0) + m_blk) * 64
        return masks.ap[:, base:base + 64].unsqueeze(1).broadcast_to([128, 8, 64])

    def phase_of(ci):
        if ci < 11:
            return "preA1"
        if ci < n_ffn:
            return "preA2"
        m = ci - n_ffn
        if m < 4:
            return "preB1"
        if m < 12:
            return "preB1"
        if m < 27:
            return "preB2"
        if m < n_mix:
            return "preB3"
        return "preC"

    conv_chunk_ci = {}
    for ci, blk in enumerate(chunks):
        off = 0
        for (kind, src, kcn, col0, tag) in blk:
            if kind == "c":
                conv_chunk_ci[col0] = ci
                off += kcn * 128
                continue
            sv = W_d[src].rearrange("(kc p) o -> p kc o", p=128)
            dst = wscr[ci, :, off:off + kcn * 128].rearrange("p (kc o) -> p kc o", kc=kcn)
            if kind == "n":
                sch.dma_nosync("pool", phase_of(ci), dst, sv[:, :, col0:col0 + 128])
            else:
                for (d0, s0) in ((0, 32), (32, 0), (64, 96), (96, 64)):
                    sch.dma_nosync("pool", phase_of(ci), dst[:, :, d0:d0 + 32], sv[:, :, col0 + s0:col0 + s0 + 32])
            off += kcn * 128
    pre_tok = {k: (k, sch.dcnt[k]) for k in ("preA1", "preA2", "preB1", "preB2", "preB3", "preC")}

    class WStream:
        def __init__(self):
            self.issued = 0
            self.total = None

        def ensure(self, upto):
            while self.issued <= upto and self.issued < self.total:
                li = self.issued
                ci = li % NCH
                slot = li % NWBUF
                n = sum(k[2] for k in chunks[ci]) * 128
                sch._wait("sp", pre_tok[phase_of(ci)])
                rd = [("dram", "cvw", chunks[ci][0][3])] if chunks[ci][0][0] == "c" else []
                sch.dma("sp", "w%d" % slot, wbuf[slot].ap[:, 0:n], wscr[ci, :, 0:n], reads=rd, writes=wbuf[slot].pg())
                self.issued += 1

        def get(self, li):
            self.ensure(li + NWBUF - 1)
            return wbuf[li % NWBUF]

    WS = WStream()
    WS.total = NCH * (NT + (1 if (NS > 0 and 'nosample' not in DBG) else 0))

    lin_rr = [0]
    LIN_N = [8]

    def lin_bank():
        b = lin_rr[0] % LIN_N[0]
        lin_rr[0] += 1
        return b

    local_mark = AR.pos
    sq = [AR.alloc(T, BF16), AR.alloc(T, BF16)]
    hid = AR.alloc(FC * T, BF16)
    sg = [AR.alloc(T, F32), AR.alloc(T, F32)]
    ffn_end = AR.pos
    xin_bufs = [AR.alloc(D, F32), AR.alloc(D, F32)]
    xin_end = AR.pos
    AR.pos = local_mark
    sq_m = [AR.alloc(T, BF16), AR.alloc(T, BF16)]
    assert sq_m[0].off == sq[0].off and sq_m[1].off == sq[1].off
    tmp = [AR.alloc(T, F32) for _ in range(4)]
    tmpA, tmpB, tmpC, tmpD = tmp
    rC = AR.alloc(T, F32)
    rS = AR.alloc(T, F32)
    QT = AR.alloc(NG * HP * T, BF16)
    QT4 = QT.ap.rearrange("p (g h t) -> p g h t", g=NG, h=HP)
    vTb = AR.alloc(T, BF16)
    kvst = AR.alloc(512, F32)
    uT = AR.alloc(KC * (30 + T), BF16)
    uT3 = uT.ap.rearrange("p (c t) -> p c t", c=KC)
    cpre = AR.alloc(KC * T, F32)
    cpre3 = cpre.ap.rearrange("p (c t) -> p c t", c=KC)
    mix_end = AR.pos
    attT = Buf(arena_t[:, rC.off // 4:(rC.off + 4096) // 4].bitcast(BF16), rC.off, 2)
    assert rS.off == rC.off + 2048
    attT3 = attT.ap.rearrange("p (h t) -> p h t", h=HP)
    cT = Buf(arena_t[:, uT.off // 4:(uT.off + 8192) // 4].bitcast(BF16), uT.off, 2)
    cT3 = cT.ap.rearrange("p (c t) -> p c t", c=KC)
    mT = Buf(arena_t[:, cpre.off // 4:(cpre.off + 8192) // 4].bitcast(BF16), cpre.off, 2)
    mT3 = mT.ap.rearrange("p (k t) -> p k t", k=KC)
    o2 = cpre.off + 8192
    PT = [Buf(arena_t[:, (o2 + i * 1024) // 4:(o2 + (i + 1) * 1024) // 4].bitcast(BF16), o2 + i * 1024, 2) for i in range(4)]
    numA = Buf(arena_t[:, (o2 + 4096) // 4:(o2 + 6144) // 4], o2 + 4096, 4)
    denA = Buf(arena_t[:, (o2 + 6144) // 4:(o2 + 8192) // 4], o2 + 6144, 4)
    AR.pos = max(ffn_end, xin_end, mix_end)
    kv_mark = AR.pos
    KT = [AR.alloc(HP * 5 * 128, BF16), AR.alloc(HP * 4 * 2 * 128, BF16), AR.alloc(HP * 16 * 2 * 128, BF16)]
    VV = [AR.alloc(5 * 512, BF16), AR.alloc(4 * 2 * 512, BF16), AR.alloc(16 * 2 * 512, BF16)]
    KT0v = KT[0].ap.rearrange("p (h s k) -> p h s k", h=HP, s=5)
    KT1v = KT[1].ap.rearrange("p (h r s k) -> p h r s k", h=HP, r=4, s=2)
    KT2v = KT[2].ap.rearrange("p (h r s k) -> p h r s k", h=HP, r=16, s=2)
    VV0v = VV[0].ap.rearrange("p (s c) -> p s c", s=5)
    VV1v = VV[1].ap.rearrange("p (r s c) -> p r s c", r=4, s=2)
    VV2v = VV[2].ap.rearrange("p (r s c) -> p r s c", r=16, s=2)
    print("arena: local %d..%d (ffn %d, xin %d, mix %d), prompt total %d of %d" %
          (local_mark, kv_mark, ffn_end, xin_end, mix_end, AR.pos, AR.nbytes))

    for g in range(NG):
        op("pool", lambda e, g=g: e.memset(KT[g].ap, 0.0), [], KT[g].pg())
        op("pool", lambda e, g=g: e.memset(VV[g].ap, 0.0), [], VV[g].pg())
    cstg = [Buf(arena_t[:, (cpre.off + i * 8192) // 4:(cpre.off + (i + 1) * 8192) // 4].bitcast(BF16), cpre.off + i * 8192, 2) for i in range(2)]
    for c in range(KC):
        stg = cstg[c % 2]
        for k in range(CW):
            op("pool", lambda e, k=k, c=c, stg=stg: e.tensor_scalar(
                out=stg.ap[:, k * 128:(k + 1) * 128], in0=ident.ap, scalar1=par.ap[:, P_CW + c * CW + k:P_CW + c * CW + k + 1],
                scalar2=None, op0=ALU.mult), ident.pg() + par.pg(), stg.pg(k * 128, 128))
        sch.dma("pool", "cvw%d" % (c % 2), wscr[conv_chunk_ci[c], :, 0:CW * 128], stg.ap[:, 0:CW * 128], reads=stg.pg(0, CW * 128),
                writes=[("dram", "cvw", c)])

    def load_x_tile(src_rows_ap, n_tok, stage=None, q="sp", key="xin"):
        nst = (n_tok + 127) // 128
        for st in range(nst):
            rows = min(128, n_tok - st * 128)
            xin = (stage or xin_bufs)[st % 2] if stage is None else stage
            sch.dma(q, "%s%d" % (key, st % 2), xin.ap[0:rows, :], src_rows_ap[st * 128:st * 128 + rows, :], writes=xin.pg())
            for h in range(2):
                b = lin_bank()
                for k4 in range(4):
                    kc = 4 * h + k4
                    op("pe", lambda e, kc=kc, k4=k4, b=b, rows=rows, xin=xin: e.transpose(
                        out=psum[b][:, k4 * 128:k4 * 128 + rows], in_=xin.ap[0:rows, kc * 128:(kc + 1) * 128],
                        identity=ident.ap[0:rows, 0:rows]), xin.pg() + ident.pg(), PS(b))
                src = psum[b][:, :].rearrange("p (k t) -> p k t", k=4)[:, :, 0:rows]
                dst = XT3[:, 4 * h:4 * h + 4, st * 128:st * 128 + rows]
                if h == 0:
                    op("act", lambda e, src=src, dst=dst: e.copy(out=dst, in_=src), PS(b), XT.pg())
                else:
                    op("dve", lambda e, src=src, dst=dst: e.tensor_copy(out=dst, in_=src), PS(b), XT.pg())

    def col_stats(src3, src_buf, n, func, scale, bias, dst):
        b = 4
        for kc in range(KC):
            s = sq[kc % 2]
            op("act", lambda e, kc=kc, s=s: e.activation(out=s.ap[:, 0:n], in_=src3[:, kc, 0:n], func=func),
               src_buf.pg(kc * T, n), s.pg())
            op("pe", lambda e, kc=kc, s=s: e.matmul(psum[b][:, 0:n], lhsT=onesb.ap, rhs=s.ap[:, 0:n], start=(kc == 0), stop=(kc == KC - 1)),
               onesb.pg() + s.pg(), PS(b))
        return b

    def rms_norm_to(dst3, dst_buf, gcol, n):
        b = col_stats(XT3, XT, n, AF.Square, None, None, None)
        op("act", lambda e: e.activation(out=rstd.ap[:, 0:n], in_=psum[b][:, 0:n], func=AF.Sqrt, bias=RMS_EPS, scale=1.0 / D), PS(b), rstd.pg())
        op("dve", lambda e: e.reciprocal(out=rstd.ap[:, 0:n], in_=rstd.ap[:, 0:n]), rstd.pg(), rstd.pg())
        for kc in range(KC):
            op("dve", lambda e, kc=kc: e.scalar_tensor_tensor(
                out=dst3[:, kc, 0:n], in0=XT3[:, kc, 0:n], scalar=par.ap[:, gcol + kc:gcol + kc + 1],
                in1=rstd.ap[:, 0:n], op0=ALU.mult, op1=ALU.mult),
               XT.pg(kc * T, n) + par.pg() + rstd.pg(), dst_buf.pg(kc * T, n))

    def mm_chunk(b, wb, woff, kcn, rhs3, rhs_buf, n):
        for kc in range(kcn):
            op("pe", lambda e, kc=kc: e.matmul(psum[b][:, 0:n], lhsT=wb.ap[:, woff + kc * 128:woff + (kc + 1) * 128],
                                              rhs=rhs3[:, kc, 0:n], start=(kc == 0), stop=(kc == kcn - 1)),
               wb.pg() + rhs_buf.pg(kc * T, n), PS(b))

    def ffn(li0, n):
        hid3 = hid.ap.rearrange("p (f t) -> p f t", f=FC)
        li = li0
        for i in range(11):
            wb = WS.get(li)
            li += 1
            for j, f in enumerate((2 * i, 2 * i + 1)):
                bg, bu = lin_bank(), lin_bank()
                mm_chunk(bg, wb, (2 * j) * 1024, KC, hT3, hT, n)
                mm_chunk(bu, wb, (2 * j + 1) * 1024, KC, hT3, hT, n)
                sgt = sg[f % 2]
                op("act", lambda e, bg=bg, sgt=sgt: e.activation(out=sgt.ap[:, 0:n], in_=psum[bg][:, 0:n], func=AF.Silu),
                   PS(bg), sgt.pg())
                op("dve", lambda e, bu=bu, sgt=sgt, f=f: e.tensor_tensor(out=hid3[:, f, 0:n], in0=psum[bu][:, 0:n],
                                                                       in1=sgt.ap[:, 0:n], op=ALU.mult),
                   PS(bu) + sgt.pg(), hid.pg(f * T, n))
        for dc in range(8):
            wb = WS.get(li)
            li += 1
            b = lin_bank()
            mm_chunk(b, wb, 0, FC, hid3, hid, n)
            op("dve", lambda e, b=b, dc=dc: e.scalar_tensor_tensor(out=XT3[:, dc, 0:n], in0=psum[b][:, 0:n], scalar=0.5,
                                                                 in1=XT3[:, dc, 0:n], op0=ALU.mult, op1=ALU.add),
               PS(b) + XT.pg(dc * T, n), XT.pg(dc * T, n))
        return li

    def store_y_tile(dst_rows_ap, n_tok, stage=None):
        rms_norm_to(XT3, XT, P_FIN, n_tok)
        nst = (n_tok + 127) // 128
        for st in range(nst):
            rows = min(128, n_tok - st * 128)
            yo = xin_bufs[st % 2] if stage is None else stage
            for h in range(2):
                b = lin_bank()
                for k4 in range(4):
                    kc = 4 * h + k4
                    op("pe", lambda e, kc=kc, k4=k4, b=b: e.transpose(
                        out=psum[b][0:rows, k4 * 128:(k4 + 1) * 128], in_=XT3[:, kc, st * 128:st * 128 + rows],
                        identity=ident.ap), XT.pg(kc * T + st * 128, rows) + ident.pg(), PS(b))
                if h == 0:
                    op("act", lambda e, b=b, yo=yo: e.copy(out=yo.ap[0:rows, 0:512], in_=psum[b][0:rows, :]), PS(b), yo.pg(0, 512))
                else:
                    op("dve", lambda e, b=b, yo=yo: e.tensor_copy(out=yo.ap[0:rows, 512:1024], in_=psum[b][0:rows, :]), PS(b), yo.pg(512, 512))
            sch.dma("pool", "yo%d" % (st % 2), dst_rows_ap[st * 128:st * 128 + rows, :], yo.ap[0:rows, :], reads=yo.pg())

    def glu_to_u(li, n, dst_fn):
        for i in range(4):
            wb = WS.get(li)
            li += 1
            for j, c in enumerate((2 * i, 2 * i + 1)):
                ba, bb = lin_bank(), lin_bank()
                mm_chunk(ba, wb, (2 * j) * 1024, KC, hT3, hT, n)
                mm_chunk(bb, wb, (2 * j + 1) * 1024, KC, hT3, hT, n)
                t = tmp[c % 2]
                dap, dkeys = dst_fn(c)
                op("act", lambda e, bb=bb, t=t: e.activation(out=t.ap[:, 0:n], in_=psum[bb][:, 0:n], func=AF.Sigmoid), PS(bb), t.pg())
                op("dve", lambda e, ba=ba, t=t, dap=dap: e.tensor_tensor(out=dap, in0=psum[ba][:, 0:n], in1=t.ap[:, 0:n], op=ALU.mult),
                   PS(ba) + t.pg(), dkeys)
        return li

    def conv_ops(n):
        for k in range(CW):
            for c in range(KC):
                wcol = par.ap[:, P_CW + c * CW + k:P_CW + c * CW + k + 1]
                rd = uT.pg(c * (30 + T) + k, n) + par.pg()
                wr = cpre.pg(c * T, n)
                if k == 0:
                    bcol = par.ap[:, P_CB + c:P_CB + c + 1]
                    yield ("dve", lambda e, c=c, wcol=wcol, bcol=bcol: e.tensor_scalar(
                        out=cpre3[:, c, 0:n], in0=uT3[:, c, 0:n], scalar1=wcol, scalar2=bcol, op0=ALU.mult, op1=ALU.add), rd, wr)
                else:
                    yield ("dve", lambda e, c=c, k=k, wcol=wcol: e.scalar_tensor_tensor(
                        out=cpre3[:, c, 0:n], in0=uT3[:, c, k:k + n], scalar=wcol, in1=cpre3[:, c, 0:n],
                        op0=ALU.mult, op1=ALU.add), rd + wr, wr)

    def conv_ln_silu(n):
        b = col_stats(cpre3, cpre, n, AF.Copy, None, None, None)
        op("dve", lambda e: e.tensor_scalar(out=tmpA.ap[:, 0:n], in0=psum[b][:, 0:n], scalar1=1.0 / D, scalar2=None, op0=ALU.mult),
           PS(b), tmpA.pg())
        for c in range(KC):
            op("dve", lambda e, c=c: e.tensor_tensor(out=cpre3[:, c, 0:n], in0=cpre3[:, c, 0:n], in1=tmpA.ap[:, 0:n], op=ALU.subtract),
               cpre.pg(c * T, n) + tmpA.pg(), cpre.pg(c * T, n))
        b = col_stats(cpre3, cpre, n, AF.Square, None, None, None)
        op("act", lambda e: e.activation(out=tmpB.ap[:, 0:n], in_=psum[b][:, 0:n], func=AF.Sqrt, bias=LN_EPS, scale=1.0 / D), PS(b), tmpB.pg())
        op("dve", lambda e: e.reciprocal(out=tmpB.ap[:, 0:n], in_=tmpB.ap[:, 0:n]), tmpB.pg(), tmpB.pg())
        for c in range(KC):
            op("dve", lambda e, c=c: e.tensor_tensor(out=cpre3[:, c, 0:n], in0=cpre3[:, c, 0:n], in1=tmpB.ap[:, 0:n], op=ALU.mult),
               cpre.pg(c * T, n) + tmpB.pg(), cpre.pg(c * T, n))
            op("act", lambda e, c=c: e.activation(out=cT3[:, c, 0:n], in_=cpre3[:, c, 0:n], func=AF.Silu,
                                                 bias=par.ap[:, P_LNB + c:P_LNB + c + 1], scale=par.ap[:, P_LNG + c:P_LNG + c + 1]),
               cpre.pg(c * T, n) + par.pg(), cT.pg(c * T, n))

    def conv_pe(li, n, rhs_fn=None):
        for c in range(KC):
            wb = WS.get(li)
            li += 1
            b = lin_bank()
            for k in range(CW):
                rhs = uT3[:, c, k:k + n] if rhs_fn is None else rhs_fn(c, k)
                rkeys = uT.pg(c * (30 + T) + k, n) if rhs_fn is None else conv_rhs_keys[0]
                outv = psum[b][:, 0:n] if rhs_fn is None else psum[b][:, 0:n].rearrange("p (b t) -> p b t", t=4)
                op("pe", lambda e, k=k, b=b, wb=wb, rhs=rhs, outv=outv: e.matmul(outv, lhsT=wb.ap[:, k * 128:(k + 1) * 128], rhs=rhs,
                                                                                 start=(k == 0), stop=(k == CW - 1)), wb.pg() + rkeys, PS(b))
            op("act", lambda e, c=c, b=b: e.activation(out=cpre3[:, c, 0:n], in_=psum[b][:, 0:n], func=AF.Identity,
                                                       bias=par.ap[:, P_CB + c:P_CB + c + 1]), PS(b) + par.pg(), cpre.pg(c * T, n))
        return li

    conv_rhs_keys = [None]

    def merge_and_out(li, n):
        for dc in range(8):
            wb = WS.get(li)
            li += 1
            b1, b2, b3, b4 = lin_bank(), lin_bank(), lin_bank(), lin_bank()
            mm_chunk(b1, wb, 0, KC, hT3, hT, n)
            mm_chunk(b2, wb, 1024, KC, hT3, hT, n)
            mm_chunk(b3, wb, 2048, KC, cT3, cT, n)
            mm_chunk(b4, wb, 3072, 4, attT3, attT, n)
            op("act", lambda e, dc=dc: e.activation(out=tmpA.ap[:, 0:n], in_=psum[b1][:, 0:n], func=AF.Sigmoid,
                                                   bias=par.ap[:, P_GB + dc:P_GB + dc + 1]), PS(b1) + par.pg(), tmpA.pg())
            op("act", lambda e, dc=dc: e.activation(out=tmpB.ap[:, 0:n], in_=psum[b2][:, 0:n], func=AF.Sigmoid,
                                                   bias=par.ap[:, P_GB + 8 + dc:P_GB + 8 + dc + 1]), PS(b2) + par.pg(), tmpB.pg())
            op("dve", lambda e: e.tensor_tensor(out=tmpA.ap[:, 0:n], in0=psum[b4][:, 0:n], in1=tmpA.ap[:, 0:n], op=ALU.mult),
               PS(b4) + tmpA.pg(), tmpA.pg())
            op("dve", lambda e: e.tensor_tensor(out=tmpB.ap[:, 0:n], in0=psum[b3][:, 0:n], in1=tmpB.ap[:, 0:n], op=ALU.mult),
               PS(b3) + tmpB.pg(), tmpB.pg())
            op("dve", lambda e, dc=dc: e.tensor_tensor(out=mT3[:, dc, 0:n], in0=tmpA.ap[:, 0:n], in1=tmpB.ap[:, 0:n], op=ALU.add),
               tmpA.pg() + tmpB.pg(), mT.pg(dc * T, n))
        for i in range(2):
            wb = WS.get(li)
            li += 1
            for j in range(4):
                dc = 4 * i + j
                b = lin_bank()
                mm_chunk(b, wb, j * 1024, KC, mT3, mT, n)
                op("dve", lambda e, b=b, dc=dc: e.tensor_tensor(out=XT3[:, dc, 0:n], in0=psum[b][:, 0:n], in1=XT3[:, dc, 0:n], op=ALU.add),
                   PS(b) + XT.pg(dc * T, n), XT.pg(dc * T, n))
        return li

    rope_rr = [0]

    def rope_chunk(bp, bs, n, dst_ap, dst_keys, f32_copy=None):
        ta, tb = (tmpA, tmpB) if rope_rr[0] % 2 == 0 else (tmpC, tmpD)
        rope_rr[0] += 1
        op("act", lambda e: e.copy(out=ta.ap[:, 0:n], in_=psum[bp][:, 0:n]), PS(bp), ta.pg())
        op("act", lambda e: e.copy(out=tb.ap[:, 0:n], in_=psum[bs][:, 0:n]), PS(bs), tb.pg())
        op("dve", lambda e: e.tensor_tensor(out=ta.ap[:, 0:n], in0=ta.ap[:, 0:n], in1=rC.ap[:, 0:n], op=ALU.mult), ta.pg() + rC.pg(), ta.pg())
        op("dve", lambda e: e.tensor_tensor(out=tb.ap[:, 0:n], in0=tb.ap[:, 0:n], in1=rS.ap[:, 0:n], op=ALU.mult), tb.pg() + rS.pg(), tb.pg())
        if f32_copy is None:
            op("dve", lambda e: e.tensor_tensor(out=dst_ap, in0=src_view(ta, n), in1=src_view(tb, n), op=ALU.add),
               ta.pg() + tb.pg(), dst_keys)
        else:
            op("dve", lambda e: e.tensor_tensor(out=ta.ap[:, 0:n], in0=ta.ap[:, 0:n], in1=tb.ap[:, 0:n], op=ALU.add), ta.pg() + tb.pg(), ta.pg())
            if dst_ap is not None:
                op("act", lambda e: e.copy(out=dst_ap, in_=src_view(ta, n)), ta.pg(), dst_keys)
        return ta

    view_d = [1]

    def src_view(buf, n):
        d = view_d[0]
        if d == 1:
            return buf.ap[:, 0:n]
        return buf.ap[:, 0:n].rearrange("p (i r) -> p i r", r=d)

    kv_deferred = []

    def emit_kv_out(g, j, hp, which, srcbuf):
        kv_deferred.append(lambda: emit_kv_out_now(g, j, hp, which, srcbuf))

    def flush_kv_out(keep=0):
        while len(kv_deferred) > keep:
            kv_deferred.pop(0)()

    def emit_kv_out_now(g, j, hp, which, srcbuf):
        pos0 = j * T
        first = S - KEEP[g]
        sts = [st for st in range(4) if pos0 + st * 128 >= first]
        if not sts:
            return
        st0 = sts[0]
        b = lin_bank()
        for st in sts:
            op("pe", lambda e, st=st, b=b: e.transpose(out=psum[b][:, st * 128:(st + 1) * 128], in_=srcbuf.ap[:, st * 128:(st + 1) * 128],
                                                        identity=ident.ap), srcbuf.pg() + ident.pg(), PS(b))
        op("act", lambda e, b=b: e.copy(out=kvst.ap[:, st0 * 128:512], in_=psum[b][:, st0 * 128:512]), PS(b), kvst.pg())
        r0 = pos0 + st0 * 128 - first
        c0 = which * 512 + hp * 128
        dst = kvp_d[g][r0:r0 + len(sts) * 128, c0:c0 + 128].rearrange("(s p) c -> p s c", p=128)
        sch.dma("pool", "kvst", dst, kvst.ap[:, st0 * 128:512].rearrange("p (s c) -> p s c", c=128), reads=kvst.pg())

    def emit_conv_out(ulast):
        ul3 = ulast.ap.rearrange("p (c t) -> p c t", c=KC)
        for h in range(2):
            b = lin_bank()
            stg = tmp[2 + h]
            for k4 in range(4):
                c = 4 * h + k4
                op("pe", lambda e, c=c, k4=k4, b=b: e.transpose(out=psum[b][0:30, k4 * 128:(k4 + 1) * 128], in_=ul3[:, c, 0:30],
                                                                  identity=ident.ap), ulast.pg() + ident.pg(), PS(b))
            op("act", lambda e, b=b, stg=stg: e.copy(out=stg.ap[0:30, :], in_=psum[b][0:30, :]), PS(b), stg.pg())
            sch.dma("pool", "convo%d" % h, convp_d[:, h * 512:(h + 1) * 512], stg.ap[0:30, :], reads=stg.pg())

    cpy_list = []
    for g in range(NG):
        for b in range(NB):
            cpy_list.append((kvs_d[g][b, 0:WINS[g] - 4, :], c_d[g][b, 4:WINS[g], :]))
    for b in range(NB):
        cpy_list.append((convs_d[b, 0:26, :], sconv_d[b, 4:30, :]))

    def emit_copies(k):
        if "nocopy" in DBG:
            return
        for _ in range(k):
            if cpy_list:
                o, i = cpy_list.pop(0)
                sch.dma_nosync(CPYQ, "cpy", o, i)

    try:
        chk('s_pre')
        li = 0
        per_tile_copies = (len(cpy_list) + max(NT, 1) - 1) // max(NT, 1)
        for j in range(NT):
            pos0 = j * T
            last_tile = (j == NT - 1)
            emit_copies(per_tile_copies)
            load_x_tile(x_d[pos0:pos0 + T, :], T)
            chk('s_load')
            rms_norm_to(hT3, hT, P_FFN1, T)
            chk('s_norm')
            li = ffn(li, T)
            chk('s_ffn1')
            rms_norm_to(hT3, hT, P_MIX, T)
            sch.dma("pool", "ropeC", rC.ap, ropeC_d[:, pos0:pos0 + T], writes=rC.pg())
            sch.dma("pool", "ropeS", rS.ap, ropeS_d[:, pos0:pos0 + T], writes=rS.pg())
            op("act", lambda e: e.copy(out=uT3[:, :, 0:30], in_=uh3), uhist.pg(), uT.pg())

            def u_dst(c):
                return uT3[:, c, 30:30 + T], uT.pg(c * (30 + T) + 30, T)

            if not last_tile:
                li = glu_to_u(li, T, u_dst)
            else:
                ul_keep = AR_ulast
                ul3k = ul_keep.ap.rearrange("p (c t) -> p c t", c=KC)
                for i in range(4):
                    wb = WS.get(li)
                    li += 1
                    for jj, c in enumerate((2 * i, 2 * i + 1)):
                        ba, bb = lin_bank(), lin_bank()
                        mm_chunk(ba, wb, (2 * jj) * 1024, KC, hT3, hT, T)
                        mm_chunk(bb, wb, (2 * jj + 1) * 1024, KC, hT3, hT, T)
                        t = tmp[c % 2]
                        op("act", lambda e, bb=bb, t=t: e.activation(out=t.ap, in_=psum[bb][:, :], func=AF.Sigmoid), PS(bb), t.pg())
                        op("dve", lambda e, ba=ba, t=t: e.tensor_tensor(out=t.ap, in0=psum[ba][:, :], in1=t.ap, op=ALU.mult), PS(ba) + t.pg(), t.pg())
                        op("act", lambda e, t=t, c=c: e.copy(out=uT3[:, c, 30:30 + T], in_=t.ap), t.pg(), uT.pg(c * (30 + T) + 30, T))
                        op("dve", lambda e, t=t, c=c: e.tensor_copy(out=ul3k[:, c, 0:30], in_=t.ap[:, T - 30:T]), t.pg(), ul_keep.pg())
            chk('s_glu')
            li = conv_pe(li, T)
            op("act", lambda e: e.copy(out=uh3, in_=uT3[:, :, T:T + 30]), uT.pg(), uhist.pg())
            cgen = iter(())
            conv_done = [False]

            def pump(k):
                for _ in range(k):
                    try:
                        eng, fn, rd, wr = next(cgen)
                    except StopIteration:
                        conv_done[0] = True
                        return
                    op(eng, fn, rd, wr)

            c_blk, m_blk = j // 4, j % 4
            for g in range(NG):
                d = DILS[g]
                view_d[0] = d
                kv_out = (pos0 + T) > (S - KEEP[g])
                pend = {}
                wb = None
                for oc in range(20):
                    if oc % 4 == 0:
                        wb = WS.get(li)
                        li += 1
                    hp, typ = oc // 5, oc % 5
                    b = lin_bank()
                    mm_chunk(b, wb, (oc % 4) * 1024, KC, hT3, hT, T)
                    pend[typ] = b
                    flush_kv_out(0)
                    pump(5)
                    if typ == 1:
                        if g == 0:
                            ta = rope_chunk(pend[0], pend[1], T, None, None, f32_copy=True)
                            for u in range(4):
                                slot = (4 * j + u) % 5
                                op("act", lambda e, u=u, slot=slot, hp=hp, ta=ta: e.copy(out=KT0v[:, hp, slot, :], in_=ta.ap[:, u * 128:(u + 1) * 128]),
                                   ta.pg(), KT[0].pg())
                            if kv_out:
                                emit_kv_out(g, j, hp, 0, ta)
                        else:
                            if g == 1:
                                dst = KT1v[:, hp, :, j % 2, :].rearrange("p r i -> p i r")
                            else:
                                dst = KT2v[:, hp, :, c_blk % 2, 32 * m_blk:32 * m_blk + 32].rearrange("p r i -> p i r")
                            if kv_out:
                                ta = rope_chunk(pend[0], pend[1], T, dst, KT[g].pg(), f32_copy=True)
                                emit_kv_out(g, j, hp, 0, ta)
                            else:
                                rope_chunk(pend[0], pend[1], T, dst, KT[g].pg())
                    elif typ == 4:
                        if g == 0:
                            dst = QT4[:, g, hp, :]
                        else:
                            dst = QT4[:, g, hp, :].rearrange("p (r i) -> p i r", r=d)
                        rope_chunk(pend[3], pend[4], T, dst, QT.pg((g * HP + hp) * T, T))
                    elif typ == 2:
                        op("act", lambda e, b=b: e.copy(out=vTb.ap, in_=psum[b][:, :]), PS(b), vTb.pg())
                        if kv_out:
                            tv = tmp[rope_rr[0] % 4]
                            op("act", lambda e, b=b, tv=tv: e.copy(out=tv.ap, in_=psum[b][:, :]), PS(b), tv.pg())
                            emit_kv_out(g, j, hp, 1, tv)
                        def vtrans(g=g, hp=hp, d=d):
                            if g < 2:
                                tb = lin_bank()
                                pb = psum[tb][:, :].bitcast(BF16)
                                for u in range(4):
                                    if g == 0:
                                        insl = vTb.ap[:, u * 128:(u + 1) * 128]
                                    else:
                                        insl = vTb.ap.rearrange("p (i r) -> p r i", r=d)[:, u, :]
                                    op("pe", lambda e, insl=insl, u=u, pb=pb: e.transpose(out=pb[:, u * 128:(u + 1) * 128], in_=insl, identity=identb.ap),
                                       vTb.pg() + identb.pg(), PS(tb))
                                if g == 0:
                                    for u in range(4):
                                        slot = (4 * j + u) % 5
                                        op("act", lambda e, u=u, slot=slot, hp=hp, pb=pb: e.copy(
                                            out=VV0v[:, slot, hp * 128:(hp + 1) * 128], in_=pb[:, u * 128:(u + 1) * 128]), PS(tb), VV[0].pg())
                                else:
                                    op("act", lambda e, hp=hp, pb=pb: e.copy(
                                        out=VV1v[:, :, j % 2, hp * 128:(hp + 1) * 128], in_=pb[:, 0:512].rearrange("p (r c) -> p r c", r=4)),
                                       PS(tb), VV[1].pg())
                            else:
                                vsrc = vTb.ap.rearrange("p (i r) -> p r i", r=d)
                                for half in range(2):
                                    tb = lin_bank()
                                    pb = psum[tb][:, :].bitcast(BF16)
                                    for r8 in range(8):
                                        r = half * 8 + r8
                                        op("pe", lambda e, r=r, r8=r8, pb=pb: e.transpose(out=pb[0:32, r8 * 128:(r8 + 1) * 128], in_=vsrc[:, r, :], identity=identb.ap),
                                           vTb.pg() + identb.pg(), PS(tb))
                                    op("act", lambda e, half=half, hp=hp, pb=pb: e.copy(
                                        out=VV2v[32 * m_blk:32 * m_blk + 32, half * 8:half * 8 + 8, c_blk % 2, hp * 128:(hp + 1) * 128],
                                        in_=pb[0:32, :].rearrange("p (r c) -> p r c", r=8)), PS(tb), VV[2].pg())

                        kv_deferred.append(vtrans)
            flush_kv_out(0)
            view_d[0] = 1
            chk('s_proj')
            while not conv_done[0]:
                pump(16)
            conv_ln_silu(T)
            chk('s_ln')
            if last_tile:
                emit_conv_out(AR_ulast)
            def units_of(g, sb):
                return [sb] if g < 2 else [4 * sb + k for k in range(4)]

            def kv_slices(hp, g, u, s, pc):
                if g == 0:
                    slot = (4 * j + u - 1 + pc) % 5
                    return (KT0v[s * 64:(s + 1) * 64, hp, slot, :], QT4[s * 64:(s + 1) * 64, g, hp, u * 128:(u + 1) * 128],
                            VV0v[:, slot, hp * 128 + s * 64:hp * 128 + (s + 1) * 64])
                if g == 1:
                    sl = (j + 1 + pc) % 2
                    return (KT1v[s * 64:(s + 1) * 64, hp, u, sl, :], QT4[s * 64:(s + 1) * 64, g, hp, u * 128:(u + 1) * 128],
                            VV1v[:, u, sl, hp * 128 + s * 64:hp * 128 + (s + 1) * 64])
                sl = (c_blk + 1 + pc) % 2
                return (KT2v[s * 64:(s + 1) * 64, hp, u, sl, :], QT4[s * 64:(s + 1) * 64, g, hp, u * 32:(u + 1) * 32],
                        VV2v[:, u, sl, hp * 128 + s * 64:hp * 128 + (s + 1) * 64])

            def colof(g, sbi, ui, pc):
                nun = 1 if g < 2 else 4
                nq = 128 if g < 2 else 32
                return ((sbi * nun + ui) * 2 + pc) * nq

            steps = [(hp, g, sp) for hp in range(HP) for g in range(NG) for sp in range(2)]

            def scores(k):
                hp, g, sp = steps[k]
                nq = 128 if g < 2 else 32
                for s in range(2):
                    bsx = 2 * (k % 2) + s
                    for sbi in range(2):
                        for ui, u in enumerate(units_of(g, 2 * sp + sbi)):
                            for pc in range(2):
                                kblk, qsl, _ = kv_slices(hp, g, u, s, pc)
                                col = colof(g, sbi, ui, pc)
                                op("pe", lambda e, kblk=kblk, qsl=qsl, col=col, bsx=bsx, nq=nq: e.matmul(
                                    psum[bsx][:, col:col + nq], lhsT=kblk, rhs=qsl, start=True, stop=True),
                                   KT[g].pg() + QT.pg((g * HP + hp) * T, T), PS(bsx))

            def softmax_part(k):
                hp, g, sp = steps[k]
                for s in range(2):
                    bsx = 2 * (k % 2) + s
                    pt = PT[2 * (k % 2) + s]
                    op("act", lambda e, bsx=bsx, pt=pt: e.activation(out=pt.ap, in_=psum[bsx][:, :], func=AF.Exp, scale=0.125), PS(bsx), pt.pg())
                    if g < 2:
                        for sbi in range(2):
                            sb = 2 * sp + sbi
                            first = (j == 0 and sb == 0) if g == 0 else (j == 0)
                            base = 256 if first else 0
                            pv2 = pt.ap[:, sbi * 256:(sbi + 1) * 256]
                            op("dve", lambda e, pv2=pv2, base=base: e.tensor_tensor(out=pv2, in0=pv2, in1=masks.ap[:, base:base + 256], op=ALU.mult),
                               pt.pg() + masks.pg(), pt.pg())
                    else:
                        mv = mask_view(g, c_blk == 0, m_blk)
                        pv3 = pt.ap.rearrange("p (a b) -> p a b", a=8)
                        op("dve", lambda e, pv3=pv3, mv=mv: e.tensor_tensor(out=pv3, in0=pv3, in1=mv, op=ALU.mult),
                           pt.pg() + masks.pg(), pt.pg())

            def nd_banks(hp, g):
                return (4, 5) if (hp * NG + g) % 2 == 0 else (6, 7)

            def pv(k):
                hp, g, sp = steps[k]
                nq = 128 if g < 2 else 32
                bn, bd = nd_banks(hp, g)
                for s in range(2):
                    pt = PT[2 * (k % 2) + s]
                    tp = None if s == 0 else (0, 64)
                    for sbi in range(2):
                        for ui, u in enumerate(units_of(g, 2 * sp + sbi)):
                            for pc in range(2):
                                _, _, vblk = kv_slices(hp, g, u, s, pc)
                                col = colof(g, sbi, ui, pc)
                                ocol = u * nq
                                op("pe", lambda e, vblk=vblk, pt=pt, col=col, ocol=ocol, s=s, pc=pc, tp=tp, bn=bn, nq=nq: e.matmul(
                                    psum[bn][s * 64:(s + 1) * 64, ocol:ocol + nq], lhsT=vblk, rhs=pt.ap[:, col:col + nq],
                                    start=(pc == 0), stop=(pc == 1), tile_position=tp),
                                   VV[g].pg() + pt.pg(), PS(bn))
                                op("pe", lambda e, pt=pt, col=col, ocol=ocol, s=s, pc=pc, tp=tp, bd=bd, nq=nq: e.matmul(
                                    psum[bd][s * 64:(s + 1) * 64, ocol:ocol + nq], lhsT=onesb.ap[:, 0:64], rhs=pt.ap[:, col:col + nq],
                                    start=(pc == 0), stop=(pc == 1), tile_position=tp),
                                   onesb.pg() + pt.pg(), PS(bd))

            def evac(hp, g):
                d = DILS[g]
                bn, bd = nd_banks(hp, g)
                for (bsrc, accb) in ((bn, numA), (bd, denA)):
                    if g == 0:
                        op("act", lambda e, bsrc=bsrc, accb=accb: e.copy(out=accb.ap, in_=psum[bsrc][:, :]), PS(bsrc), accb.pg())
                    else:
                        dstv = accb.ap.rearrange("p (i r) -> p r i", r=d)
                        srcv = psum[bsrc][:, :].rearrange("p (r i) -> p r i", r=d)
                        op("dve", lambda e, dstv=dstv, srcv=srcv: e.tensor_tensor(out=dstv, in0=srcv, in1=dstv, op=ALU.add),
                           PS(bsrc) + accb.pg(), accb.pg())

            def finalize(hp):
                op("dve", lambda e: e.reciprocal(out=denA.ap, in_=denA.ap), denA.pg(), denA.pg())
                op("dve", lambda e, hp=hp: e.tensor_tensor(out=attT3[:, hp, :], in0=numA.ap, in1=denA.ap, op=ALU.mult),
                   numA.pg() + denA.pg(), attT.pg(hp * T, T))

            pend_fin = []
            scores(0)
            for k, (hp, g, sp) in enumerate(steps):
                softmax_part(k)
                while pend_fin:
                    finalize(pend_fin.pop(0))
                if k + 1 < len(steps):
                    scores(k + 1)
                pv(k)
                if sp == 1:
                    evac(hp, g)
                    if g == NG - 1:
                        pend_fin.append(hp)
            while pend_fin:
                finalize(pend_fin.pop(0))
            chk('s_att')
            li = merge_and_out(li, T)
            chk('s_merge')
            rms_norm_to(hT3, hT, P_FFN2, T)
            li = ffn(li, T)
            store_y_tile(y_d[pos0:pos0 + T, :], T)
            chk('s_tile')

        emit_copies(len(cpy_list))
        if NS > 0 and 'nosample' not in DBG:
            n = NS
            AR.pos = kv_mark
            ucs = AR.alloc(KC * NB * 34, BF16)
            ucs4 = ucs.ap.rearrange("p (c b t) -> p c b t", c=KC, b=NB)
            uS = AR.alloc(KC * 64, F32)
            uS3 = uS.ap.rearrange("p (c t) -> p c t", c=KC)
            sstage = AR.alloc(D, F32)
            KVs = [AR.alloc(1024, F32) for _ in range(NG)]
            Qs = [AR.alloc(512, F32) for _ in range(NG)]
            Qb = AR.alloc(NG * 512, BF16)
            KVc = [AR.alloc(1024, F32), AR.alloc(4 * 1024, F32), AR.alloc(4 * 1024, F32)]
            Qbc = AR.alloc(4 * NG * 512, BF16)
            pvb = [AR.alloc(512, BF16), AR.alloc(512, BF16)]
            num_new = AR.alloc(512, F32)
            att_s = AR.alloc(512, F32)
            s_all = AR.alloc(96, F32)
            p_all = AR.alloc(96, F32)
            p_bf = AR.alloc(96, BF16)
            s_new = AR.alloc(48, F32)
            p_new = AR.alloc(48, F32)
            den_new = AR.alloc(8, F32)
            zsel = AR.alloc(127, BF16)
            ipad = AR.alloc(67, F32)
            vmask = AR.alloc(4, F32)
            g0mask = AR.alloc(32, F32)
            print("arena used (sample phase):", AR.pos, "of", AR.nbytes)
            sch.dma("pool", "c0", zsel.ap, zsel_d[:, :], writes=zsel.pg())
            sch.dma("pool", "c0", ipad.ap[0:64, :], ipad_d[:, :], writes=ipad.pg())
            sch.dma("pool", "c0", vmask.ap[0:64, :], vmask_d[:, :], writes=vmask.pg())
            sch.dma("pool", "c0", g0mask.ap, g0mask_d[:, :], writes=g0mask.pg())

            load_x_tile(xs_d, n)
            rms_norm_to(hT3, hT, P_FFN1, n)
            li = ffn(li, n)
            rms_norm_to(hT3, hT, P_MIX, n)
            sch.dma("pool", "ropeC", rC.ap[:, 0:64], ropeC_d[:, S:S + 64], writes=rC.pg())
            sch.dma("pool", "ropeS", rS.ap[:, 0:64], ropeS_d[:, S:S + 64], writes=rS.pg())
            li = glu_to_u(li, n, lambda c: (uS3[:, c, 0:n], uS.pg(c * 64, n)))
            for q0 in range(0, NB, 4):
                nb = min(4, NB - q0)
                rows = nb * 30
                sch.dma("sp", "sst", sstage.ap[0:rows, :], sconv_d[q0:q0 + nb, :, :].rearrange("b r c -> (b r) c"), writes=sstage.pg())
                for h in range(2):
                    b = lin_bank()
                    for k4 in range(4):
                        kc = 4 * h + k4
                        op("pe", lambda e, kc=kc, k4=k4, b=b, rows=rows: e.transpose(
                            out=psum[b][:, k4 * 128:k4 * 128 + rows], in_=sstage.ap[0:rows, kc * 128:(kc + 1) * 128],
                            identity=ident.ap[0:rows, 0:rows]), sstage.pg() + ident.pg(), PS(b))
                    for k4 in range(4):
                        kc = 4 * h + k4
                        op("dve", lambda e, kc=kc, k4=k4, b=b, nb=nb, q0=q0, rows=rows: e.tensor_copy(
                            out=ucs4[:, kc, q0:q0 + nb, 0:30], in_=psum[b][:, k4 * 128:k4 * 128 + rows].rearrange("p (b r) -> p b r", b=nb)),
                           PS(b), ucs.pg())
            op("dve", lambda e: e.tensor_copy(out=ucs4[:, :, :, 30:34], in_=uS3[:, :, 0:n].rearrange("p c (b t) -> p c b t", t=4)),
               uS.pg(), ucs.pg())
            for h in range(2):
                b = lin_bank()
                for k4 in range(4):
                    c = 4 * h + k4
                    op("pe", lambda e, c=c, k4=k4, b=b: e.transpose(out=psum[b][0:n, k4 * 128:(k4 + 1) * 128], in_=uS3[:, c, 0:n],
                                                                      identity=ident.ap), uS.pg() + ident.pg(), PS(b))
                op("act", lambda e, b=b, h=h: e.copy(out=sstage.ap[0:n, h * 512:(h + 1) * 512], in_=psum[b][0:n, :]), PS(b), sstage.pg())
            for bb in range(NB):
                sch.dma("pool", "convs", convs_d[bb, 26:30, :], sstage.ap[4 * bb:4 * bb + 4, :], reads=sstage.pg())
            conv_rhs_keys[0] = ucs.pg()
            li = conv_pe(li, n, rhs_fn=lambda c, k: ucs4[:, c, :, k:k + 4])
            view_d[0] = 1
            for g in range(NG):
                pend = {}
                wb = None
                for oc in range(20):
                    if oc % 4 == 0:
                        wb = WS.get(li)
                        li += 1
                    hp, typ = oc // 5, oc % 5
                    b = lin_bank()
                    mm_chunk(b, wb, (oc % 4) * 1024, KC, hT3, hT, n)
                    pend[typ] = b
                    if typ in (1, 4, 2):
                        if typ == 2:
                            ta = tmp[rope_rr[0] % 4]
                            op("act", lambda e, b=b, ta=ta: e.copy(out=ta.ap[:, 0:n], in_=psum[b][:, 0:n]), PS(b), ta.pg())
                        else:
                            ta = rope_chunk(pend[typ - 1], pend[typ], n, None, None, f32_copy=True)
                        tb = lin_bank()
                        op("pe", lambda e, tb=tb, ta=ta: e.transpose(out=psum[tb][0:n, 0:128], in_=ta.ap[:, 0:n], identity=ident.ap), ta.pg() + ident.pg(), PS(tb))
                        if typ == 1:
                            dstb, dsl = KVs[g], slice(hp * 128, (hp + 1) * 128)
                        elif typ == 2:
                            dstb, dsl = KVs[g], slice(512 + hp * 128, 512 + (hp + 1) * 128)
                        else:
                            dstb, dsl = Qs[g], slice(hp * 128, (hp + 1) * 128)
                        op("act", lambda e, tb=tb, dstb=dstb, dsl=dsl: e.copy(out=dstb.ap[0:n, dsl], in_=psum[tb][0:n, 0:128]), PS(tb), dstb.pg())
                for bb in range(NB):
                    sch.dma("pool", "kvsn", kvs_d[g][bb, WINS[g] - 4:WINS[g], :], KVs[g].ap[4 * bb:4 * bb + 4, :], reads=KVs[g].pg())
                op("dve", lambda e, g=g: e.tensor_copy(out=Qb.ap[0:n, g * 512:(g + 1) * 512], in_=Qs[g].ap[0:n, :]), Qs[g].pg(), Qb.pg())
            sch.dma("pool", "qscr", qscr[:, :], Qb.ap[0:n, :], reads=Qb.pg(), writes=[("dram", "qscr")])
            conv_ln_silu(n)

            LIN_N[0] = 4
            first_mm = [True]
            for bb in range(NB):
                sch.dma("sp", "kvc0", KVc[0].ap, c_d[0][bb, :, :], writes=KVc[0].pg())
                sch.dma("sp", "kvc1", KVc[1].ap.rearrange("p (t c) -> p t c", t=4), c_d[1][bb].rearrange("(i t) c -> i t c", t=4), writes=KVc[1].pg())
                sch.dma("sp", "kvc2", KVc[2].ap.rearrange("p (t c) -> p t c", t=4), c_d[2][bb].rearrange("(i r) c -> i r c", r=16)[:, 0:4, :], writes=KVc[2].pg())
                qsrc = qscr[4 * bb:4 * bb + 4, :].rearrange("t c -> (t c)").partition_broadcast(128)
                sch.dma("pool", "qbc", Qbc.ap, qsrc, reads=[("dram", "qscr")], writes=Qbc.pg())
                for g in range(NG):
                    for t in range(4):
                        pi = g * 4 + t
                        ksl = KVc[0].ap[:, 0:512] if g == 0 else KVc[g].ap[:, t * 1024:t * 1024 + 512]
                        qsl = Qbc.ap[:, (t * NG + g) * 512:(t * NG + g + 1) * 512]
                        tq = tmp[pi % 2]
                        op("dve", lambda e, ksl=ksl, qsl=qsl, tq=tq: e.tensor_tensor(out=tq.ap, in0=ksl, in1=qsl, op=ALU.mult),
                           KVc[g].pg() + Qbc.pg(), tq.pg())
                        op("dve", lambda e, pi=pi, tq=tq: e.tensor_reduce(out=s_all.ap[:, pi * 8:(pi + 1) * 8], in_=tq.ap.rearrange("p (h e) -> p h e", h=8),
                                                                          axis=AX.X, op=ALU.add), tq.pg(), s_all.pg())
                op("act", lambda e: e.activation(out=p_all.ap, in_=s_all.ap, func=AF.Exp, scale=0.125), s_all.pg(), p_all.pg())
                op("dve", lambda e: e.tensor_tensor(out=p_all.ap[:, 0:32], in0=p_all.ap[:, 0:32], in1=g0mask.ap, op=ALU.mult),
                   p_all.pg() + g0mask.pg(), p_all.pg())
                op("act", lambda e: e.copy(out=p_bf.ap, in_=p_all.ap), p_all.pg(), p_bf.pg())
                for g in range(NG):
                    for t in range(4):
                        pi = g * 4 + t
                        tok = 4 * bb + t
                        vsl = KVc[0].ap[:, 512:1024] if g == 0 else KVc[g].ap[:, t * 1024 + 512:(t + 1) * 1024]
                        pb_ = pvb[pi % 2]
                        pin = p_all.ap[:, pi * 8:(pi + 1) * 8].unsqueeze(2).broadcast_to([128, 8, 64])
                        op("pool", lambda e, vsl=vsl, pb_=pb_, pin=pin: e.tensor_tensor(out=pb_.ap.rearrange("p (h e) -> p h e", h=8),
                                                                                       in0=vsl.rearrange("p (h e) -> p h e", h=8), in1=pin, op=ALU.mult),
                           KVc[g].pg() + p_all.pg(), pb_.pg())
                        last = (bb == NB - 1 and pi == 11)
                        fm = first_mm[0]
                        first_mm[0] = False
                        lsel = zsel.ap[:, 63 - tok:63 - tok + n]
                        op("pe", lambda e, lsel=lsel, pb_=pb_, fm=fm, last=last: e.matmul(psum[6][0:n, :], lhsT=lsel, rhs=pb_.ap, start=fm, stop=last),
                           zsel.pg() + pb_.pg(), PS(6))
                        op("pe", lambda e, lsel=lsel, pi=pi, fm=fm, last=last: e.matmul(psum[7][0:n, 0:8], lhsT=lsel, rhs=p_bf.ap[:, pi * 8:(pi + 1) * 8],
                                                                                       start=fm, stop=last), zsel.pg() + p_bf.pg(), PS(7))
            vsh = [tmpA, tmpB, tmpD]
            for dl in range(1, 4):
                lsh = ipad.ap[0:n, 3 - dl:3 - dl + n]
                op("pe", lambda e, lsh=lsh: e.matmul(psum[0][0:n, :], lhsT=lsh, rhs=KVs[0].ap[0:n, 0:512], start=True, stop=True), ipad.pg() + KVs[0].pg(), PS(0))
                op("pe", lambda e, lsh=lsh: e.matmul(psum[1][0:n, :], lhsT=lsh, rhs=KVs[0].ap[0:n, 512:1024], start=True, stop=True), ipad.pg() + KVs[0].pg(), PS(1))
                col = 2 + dl
                op("dve", lambda e: e.tensor_tensor(out=tmpC.ap[0:n, :], in0=psum[0][0:n, :], in1=Qs[0].ap[0:n, :], op=ALU.mult), PS(0) + Qs[0].pg(), tmpC.pg())
                op("dve", lambda e, col=col: e.tensor_reduce(out=s_new.ap[0:n, col * 8:(col + 1) * 8], in_=tmpC.ap[0:n, :].rearrange("p (h e) -> p h e", h=8),
                                                             axis=AX.X, op=ALU.add), tmpC.pg(), s_new.pg())
                vst = vsh[dl - 1]
                op("act", lambda e, vst=vst: e.copy(out=vst.ap[0:n, :], in_=psum[1][0:n, :]), PS(1), vst.pg())
            for g in range(NG):
                op("dve", lambda e, g=g: e.tensor_tensor(out=tmpC.ap[0:n, :], in0=KVs[g].ap[0:n, 0:512], in1=Qs[g].ap[0:n, :], op=ALU.mult),
                   KVs[g].pg() + Qs[g].pg(), tmpC.pg())
                op("dve", lambda e, g=g: e.tensor_reduce(out=s_new.ap[0:n, g * 8:(g + 1) * 8], in_=tmpC.ap[0:n, :].rearrange("p (h e) -> p h e", h=8),
                                                         axis=AX.X, op=ALU.add), tmpC.pg(), s_new.pg())
            op("act", lambda e: e.activation(out=p_new.ap[0:n, :], in_=s_new.ap[0:n, :], func=AF.Exp, scale=0.125), s_new.pg(), p_new.pg())
            for dl in range(1, 4):
                col = 2 + dl
                op("dve", lambda e, col=col, dl=dl: e.tensor_scalar(out=p_new.ap[0:n, col * 8:(col + 1) * 8], in0=p_new.ap[0:n, col * 8:(col + 1) * 8],
                                                                   scalar1=vmask.ap[0:n, dl:dl + 1], scalar2=None, op0=ALU.mult),
                   p_new.pg() + vmask.pg(), p_new.pg())
            vsrcs = [(KVs[0].ap[0:n, 512:1024], KVs[0]), (KVs[1].ap[0:n, 512:1024], KVs[1]), (KVs[2].ap[0:n, 512:1024], KVs[2]),
                     (tmpA.ap[0:n, :], tmpA), (tmpB.ap[0:n, :], tmpB), (tmpD.ap[0:n, :], tmpD)]
            nn3 = num_new.ap[0:n, :].rearrange("p (h e) -> p h e", h=8)
            for ti, (vap, vbuf) in enumerate(vsrcs):
                pin = p_new.ap[0:n, ti * 8:(ti + 1) * 8].unsqueeze(2).broadcast_to([n, 8, 64])
                v3 = vap.rearrange("p (h e) -> p h e", h=8)
                if ti == 0:
                    op("dve", lambda e, v3=v3, pin=pin: e.tensor_tensor(out=nn3, in0=v3, in1=pin, op=ALU.mult), vbuf.pg() + p_new.pg(), num_new.pg())
                    op("dve", lambda e: e.tensor_copy(out=den_new.ap[0:n, :], in_=p_new.ap[0:n, 0:8]), p_new.pg(), den_new.pg())
                else:
                    t3 = tmpC.ap[0:n, :].rearrange("p (h e) -> p h e", h=8)
                    op("dve", lambda e, v3=v3, pin=pin, t3=t3: e.tensor_tensor(out=t3, in0=v3, in1=pin, op=ALU.mult), vbuf.pg() + p_new.pg(), tmpC.pg())
                    op("dve", lambda e: e.tensor_tensor(out=num_new.ap[0:n, :], in0=num_new.ap[0:n, :], in1=tmpC.ap[0:n, :], op=ALU.add),
                       num_new.pg() + tmpC.pg(), num_new.pg())
                    op("dve", lambda e, ti=ti: e.tensor_tensor(out=den_new.ap[0:n, :], in0=den_new.ap[0:n, :], in1=p_new.ap[0:n, ti * 8:(ti + 1) * 8], op=ALU.add),
                       den_new.pg() + p_new.pg(), den_new.pg())
            op("dve", lambda e: e.tensor_tensor(out=num_new.ap[0:n, :], in0=psum[6][0:n, :], in1=num_new.ap[0:n, :], op=ALU.add), PS(6) + num_new.pg(), num_new.pg())
            op("dve", lambda e: e.tensor_tensor(out=den_new.ap[0:n, :], in0=psum[7][0:n, 0:8], in1=den_new.ap[0:n, :], op=ALU.add), PS(7) + den_new.pg(), den_new.pg())
            op("dve", lambda e: e.reciprocal(out=den_new.ap[0:n, :], in_=den_new.ap[0:n, :]), den_new.pg(), den_new.pg())
            rin = den_new.ap[0:n, :].unsqueeze(2).broadcast_to([n, 8, 64])
            op("dve", lambda e: e.tensor_tensor(out=att_s.ap[0:n, :].rearrange("p (h e) -> p h e", h=8), in0=nn3, in1=rin, op=ALU.mult),
               num_new.pg() + den_new.pg(), att_s.pg())
            for hp in range(HP):
                tb = lin_bank()
                op("pe", lambda e, hp=hp, tb=tb: e.transpose(out=psum[tb][:, 0:n], in_=att_s.ap[0:n, hp * 128:(hp + 1) * 128], identity=ident.ap[0:n, 0:n]),
                   att_s.pg() + ident.pg(), PS(tb))
                op("act", lambda e, hp=hp, tb=tb: e.copy(out=attT3[:, hp, 0:n], in_=psum[tb][:, 0:n]), PS(tb), attT.pg(hp * T, n))
            LIN_N[0] = 8
            li = merge_and_out(li, n)
            rms_norm_to(hT3, hT, P_FFN2, n)
            li = ffn(li, n)
            store_y_tile(ys_d, n)

    except _Stop as ex:
        print('STOPPED AT', ex)
    sch.finish("sp")
    return nc, stack


def _consts(S):
    half = 32
    inv_freq = (np.float32(10000.0) ** (-(np.arange(half, dtype=np.float32) * np.float32(2.0) / np.float32(64)))).astype(np.float32)
    pos = np.concatenate([np.arange(S, dtype=np.float32), np.float32(PAST) + (np.arange(64) % 4).astype(np.float32)])
    ang = (pos[None, :] * inv_freq[:, None]).astype(np.float32)
    cos = np.cos(ang).astype(np.float32)
    sin = np.sin(ang).astype(np.float32)
    p = np.arange(128)
    f = p % 32
    sign = np.where((p % 64) < 32, -1.0, 1.0).astype(np.float32)
    ropeC = cos[f, :]
    ropeS = sin[f, :] * sign[:, None]
    kj = np.arange(128)[:, None]
    masks = np.zeros((128, NMASK), np.float32)
    qi = np.arange(128)[None, :]
    masks[:, 0:128] = (kj >= qi)
    masks[:, 128:256] = (kj <= qi)
    masks[:, 384:512] = (kj <= qi)
    qi32 = np.arange(32)[None, :]
    for var in range(2):
        for mb in range(4):
            base = 512 + (4 * var + mb) * 64
            if var == 0:
                masks[:, base:base + 32] = (kj >= 32 * mb + qi32)
            masks[:, base + 32:base + 64] = (kj <= 32 * mb + qi32)
    zsel = np.zeros((128, 127), np.float32)
    zsel[:, 63] = 1.0
    ipad = np.zeros((64, 67), np.float32)
    ipad[np.arange(64), np.arange(64) + 3] = 1.0
    tok = np.arange(64)
    vmask = ((tok % 4)[:, None] >= np.arange(4)[None, :]).astype(np.float32)
    g0mask = np.zeros((128, 4, 8), np.float32)
    for t in range(4):
        g0mask[:, t, :] = (np.arange(128) >= t)[:, None]
    return dict(ropeC=np.ascontiguousarray(ropeC), ropeS=np.ascontiguousarray(ropeS),
                masks=masks.astype(ml_dtypes.bfloat16), ident=np.eye(128, dtype=np.float32),
                zsel=zsel.astype(ml_dtypes.bfloat16), ipad=ipad, vmask=vmask, g0mask=g0mask.reshape(128, 32))


def _pack_params(inp):
    def col(v):
        return np.asarray(v, np.float32).reshape(8, 128).T
    par = np.zeros((128, NPAR), np.float32)
    par[:, P_FFN1:P_FFN1 + 8] = col(inp["ffn1_norm"][0])
    par[:, P_MIX:P_MIX + 8] = col(inp["mix_norm"][0])
    par[:, P_FFN2:P_FFN2 + 8] = col(inp["ffn2_norm"][0])
    par[:, P_FIN:P_FIN + 8] = col(inp["final_norm"])
    par[:, P_CB:P_CB + 8] = col(inp["conv_b"][0])
    par[:, P_LNG:P_LNG + 8] = col(inp["conv_ln_g"][0])
    par[:, P_LNB:P_LNB + 8] = col(inp["conv_ln_b"][0])
    gb = np.asarray(inp["gate_bias"][0], np.float32)
    par[:, P_GB:P_GB + 8] = col(gb[:D])
    par[:, P_GB + 8:P_GB + 16] = col(gb[D:])
    cw = np.asarray(inp["conv_w"][0], np.float32)
    par[:, P_CW:P_CW + 8 * CW] = cw.T.reshape(8, 128, CW).transpose(1, 0, 2).reshape(128, 8 * CW)
    return par


def run(inp, ncores=NCORES, trace=False):
    xp = np.asarray(inp["x_prompt"], np.float32)
    xsm = np.asarray(inp["x_sample"], np.float32)
    B, S, _ = xp.shape
    NBT = xsm.shape[0]
    assert B == ncores and NBT % ncores == 0
    NB = NBT // ncores
    nc, stack = build(S, NB)
    consts = _consts(S)
    par = _pack_params(inp)
    shared = {k: np.ascontiguousarray(np.asarray(inp[k], np.float32)[0]) for k in WSHAPES}
    shared["params"] = par
    shared.update(consts)
    caches = [np.asarray(inp["cache_kv_w128"], np.float32)[0], np.asarray(inp["cache_kv_w512"], np.float32)[0],
              np.asarray(inp["cache_kv_w2048"], np.float32)[0]]
    sconv = np.asarray(inp["state_conv"], np.float32)[0]
    in_maps = []
    for c in range(ncores):
        m = dict(shared)
        m["x"] = np.ascontiguousarray(xp[c])
        m["xs"] = np.ascontiguousarray(xsm[c * NB:(c + 1) * NB].reshape(NB * 4, D))
        for g, w in enumerate(WINS):
            m["c%d" % w] = np.ascontiguousarray(caches[g][c * NB:(c + 1) * NB].reshape(NB, w, 1024))
        m["sconv"] = np.ascontiguousarray(sconv[c * NB:(c + 1) * NB])
        in_maps.append(m)
    res = run_bass_kernel_spmd(nc, in_maps, core_ids=list(range(ncores)), trace=trace)
    stack.close()
    R = res.results
    KEEP = [min(w, S) for w in WINS]
    y = np.stack([R[c]["y"] for c in range(ncores)], 0)
    ys = np.concatenate([R[c]["ys"].reshape(NB, 4, D) for c in range(ncores)], 0)
    kvp = [np.stack([R[c]["kvp%d" % g].reshape(KEEP[g], 2, 8, 64) for c in range(ncores)], 0)[None] for g in range(NG)]
    convp = np.stack([R[c]["convp"] for c in range(ncores)], 0)[None]
    kvs = [np.concatenate([R[c]["kvs%d" % g].reshape(NB, WINS[g], 2, 8, 64) for c in range(ncores)], 0)[None] for g in range(NG)]
    convs = np.concatenate([R[c]["convs"] for c in range(ncores)], 0)[None]
    out = (y, ys, kvp[0], kvp[1], kvp[2], convp, kvs[0], kvs[1], kvs[2], convs)
    return out, res


def kernel(**inputs):
    out, _ = run(inputs)
    return tuple(np.ascontiguousarray(o.astype(np.float32, copy=False)) for o in out)
```
